# Optimizing a Trainium2 kernel written in Bass

```python
import jax
import jax.numpy as jnp
from jax import lax
import numpy as np

D_MODEL = 1024
BATCH = 4
SEQ = 8192
DEPTH = 1

DN_HEADS = 4
DN_HEAD_DIM = 128
DN_WIDTH = DN_HEADS * DN_HEAD_DIM
CONV_WIDTH = 4
CHUNK = 64
SWA_Q_HEADS = 8
SWA_KV_HEADS = 2
SWA_HEAD_DIM = 64
SWA_GROUP = SWA_Q_HEADS // SWA_KV_HEADS
SWA_WIDTH = SWA_Q_HEADS * SWA_HEAD_DIM
SWA_KV_WIDTH = SWA_KV_HEADS * SWA_HEAD_DIM
WINDOW = 128
ROPE_THETA = 500000.0
ROPE_DIM = SWA_HEAD_DIM // 4
N_BRANCH = 2
BRANCH_WIDTH = DN_WIDTH
DEEPNORM_ALPHA = (2.0 * DEPTH) ** 0.25
DEEPNORM_BETA = (8.0 * DEPTH) ** -0.25
LN_EPS = 1e-5
NORM_EPS = 1e-6

IN_SIZES = (DN_WIDTH, DN_WIDTH, DN_WIDTH,
            DN_WIDTH,
            DN_HEADS, DN_HEADS,
            SWA_WIDTH, SWA_KV_WIDTH, SWA_KV_WIDTH,
            SWA_WIDTH,
            D_MODEL, D_MODEL)
IN_COLS = sum(IN_SIZES)
IN_OFFSETS = tuple(int(o) for o in np.cumsum(IN_SIZES)[:-1])

kernel_name = "hybrid_gdn_swa_sink_deepnorm"


def causal_short_conv(x, w):
    return lax.conv_general_dilated(
        x, w[:, None, :].astype(x.dtype), window_strides=(1,),
        padding=[(CONV_WIDTH - 1, 0)], dimension_numbers=("NWC", "WIO", "NWC"),
        feature_group_count=x.shape[-1])


def l2norm(t):
    t = t.astype(jnp.float32)
    return t * lax.rsqrt(jnp.sum(t * t, axis=-1, keepdims=True) + NORM_EPS)


def chunked_gated_delta_rule(q, k, v, g, beta):
    b, s, h, dk = q.shape
    dv = v.shape[-1]
    n = s // CHUNK

    def chunks(t):
        return t.reshape(b, n, CHUNK, h, -1).transpose(0, 1, 3, 2, 4)

    q, k, v = chunks(q), chunks(k), chunks(v)
    g = g.reshape(b, n, CHUNK, h).transpose(0, 1, 3, 2)
    beta = beta.reshape(b, n, CHUNK, h).transpose(0, 1, 3, 2)
    g = jnp.cumsum(g, axis=-1)

    pos = jnp.arange(CHUNK)
    causal = pos[:, None] >= pos[None, :]
    strict = pos[:, None] > pos[None, :]
    decay = jnp.where(causal, jnp.exp(jnp.where(causal, g[..., :, None] - g[..., None, :], 0.0)), 0.0)

    k_beta = k * beta[..., None]
    lower = jnp.where(strict, jnp.einsum("bnhid,bnhjd->bnhij", k_beta, k) * decay, 0.0)
    eye = jnp.eye(CHUNK, dtype=jnp.float32)
    t_inv = lax.linalg.triangular_solve(eye + lower, jnp.broadcast_to(eye, lower.shape),
                                        left_side=True, lower=True, unit_diagonal=True)
    u = jnp.einsum("bnhij,bnhjd->bnhid", t_inv, v * beta[..., None])
    w = jnp.einsum("bnhij,bnhjd->bnhid", t_inv, k_beta * jnp.exp(g)[..., None])
    intra = jnp.where(causal, jnp.einsum("bnhid,bnhjd->bnhij", q, k) * decay, 0.0)
    q_decayed = q * jnp.exp(g)[..., None]
    g_last = g[..., -1]
    k_tail = k * jnp.exp(g_last[..., None] - g)[..., None]

    def step(state, xs):
        qd_i, a_i, u_i, w_i, kt_i, gl_i = xs
        v_new = u_i - jnp.einsum("bhcd,bhde->bhce", w_i, state)
        o_i = jnp.einsum("bhcd,bhde->bhce", qd_i, state) + jnp.einsum("bhij,bhje->bhie", a_i, v_new)
        state = state * jnp.exp(gl_i)[..., None, None] + jnp.einsum("bhcd,bhce->bhde", kt_i, v_new)
        return state, o_i

    xs = tuple(jnp.moveaxis(t, 1, 0) for t in (q_decayed, intra, u, w, k_tail, g_last))
    _, out = lax.scan(step, jnp.zeros((b, h, dk, dv), jnp.float32), xs)
    return out.transpose(1, 0, 3, 2, 4).reshape(b, s, h, dv)


def rope_tables(seq):
    inv_freq = ROPE_THETA ** (-jnp.arange(0, ROPE_DIM, 2, dtype=jnp.float32) / ROPE_DIM)
    ang = jnp.arange(seq, dtype=jnp.float32)[:, None] * inv_freq[None, :]
    return jnp.cos(ang), jnp.sin(ang)


def partial_rope(t, cos, sin):
    half = ROPE_DIM // 2
    c = cos[None, :, None, :].astype(t.dtype)
    s = sin[None, :, None, :].astype(t.dtype)
    t1 = t[..., :half]
    t2 = t[..., half:ROPE_DIM]
    return jnp.concatenate([t1 * c - t2 * s, t2 * c + t1 * s, t[..., ROPE_DIM:]], axis=-1)


def banded_sink_attention(q, k, v, sinks):
    b, s, _, d = q.shape
    nb = s // WINDOW
    qb = q.reshape(b, nb, WINDOW, SWA_KV_HEADS, SWA_GROUP, d)

    def band(t):
        tb = t.reshape(b, nb, WINDOW, SWA_KV_HEADS, d)
        prev = jnp.pad(tb, ((0, 0), (1, 0), (0, 0), (0, 0), (0, 0)))[:, :-1]
        return jnp.concatenate([prev, tb], axis=2)

    kb, vb = band(k), band(v)
    scores = jnp.einsum("bnqhgd,bnkhd->bnhgqk", qb, kb).astype(jnp.float32) * (d ** -0.5)
    qpos = jnp.arange(WINDOW)[:, None] + WINDOW
    kpos = jnp.arange(2 * WINDOW)[None, :]
    dist = qpos - kpos
    in_band = (dist >= 0) & (dist < WINDOW)
    blk = jnp.arange(nb)[:, None, None]
    valid = in_band[None] & ((blk > 0) | (kpos >= WINDOW)[None])
    scores = jnp.where(valid[None, :, None, None], scores, -jnp.inf)
    sink = sinks.astype(jnp.float32).reshape(SWA_KV_HEADS, SWA_GROUP)[None, None, :, :, None, None]
    m = jnp.maximum(jnp.max(scores, axis=-1, keepdims=True), sink)
    p = jnp.exp(scores - m)
    probs = p / (jnp.sum(p, axis=-1, keepdims=True) + jnp.exp(sink - m))
    out = jnp.einsum("bnhgqk,bnkhd->bnqhgd", probs.astype(v.dtype), vb)
    return out.reshape(b, s, SWA_Q_HEADS * d)


def hybrid_layer(x, w_in, conv_w, a_log, dt_bias, dn_norm_w, sinks, w_branch, w_out, ln_g, ln_b, cos, sin):
    b, s, _ = x.shape
    h = jnp.einsum("bsd,dc->bsc", x, w_in)
    (_, _, _, dn_z, dn_b, dn_a, sw_q, sw_k, sw_v, sw_z, gate_a, gate_b) = jnp.split(h, IN_OFFSETS, axis=-1)

    qkv = jax.nn.silu(causal_short_conv(h[..., :3 * DN_WIDTH], conv_w))
    dq, dk, dv = jnp.split(qkv, 3, axis=-1)
    q_a = l2norm(dq.reshape(b, s, DN_HEADS, DN_HEAD_DIM)) * (DN_HEAD_DIM ** -0.5)
    k_a = l2norm(dk.reshape(b, s, DN_HEADS, DN_HEAD_DIM))
    v_a = dv.reshape(b, s, DN_HEADS, DN_HEAD_DIM).astype(jnp.float32)
    beta = jax.nn.sigmoid(dn_b.astype(jnp.float32))
    g = -jnp.exp(a_log.astype(jnp.float32)) * jax.nn.softplus(dn_a.astype(jnp.float32) + dt_bias.astype(jnp.float32))
    o_a = chunked_gated_delta_rule(q_a, k_a, v_a, g, beta)
    o_a = o_a * lax.rsqrt(jnp.mean(o_a * o_a, axis=-1, keepdims=True) + NORM_EPS) * dn_norm_w.astype(jnp.float32)
    y_a = (o_a.astype(x.dtype) * jax.nn.silu(dn_z.reshape(b, s, DN_HEADS, DN_HEAD_DIM))).reshape(b, s, DN_WIDTH)

    q_b = partial_rope(sw_q.reshape(b, s, SWA_Q_HEADS, SWA_HEAD_DIM), cos, sin)
    k_b = partial_rope(sw_k.reshape(b, s, SWA_KV_HEADS, SWA_HEAD_DIM), cos, sin)
    v_b = sw_v.reshape(b, s, SWA_KV_HEADS, SWA_HEAD_DIM)
    y_b = banded_sink_attention(q_b, k_b, v_b, sinks) * jax.nn.silu(sw_z)

    merged = (jax.nn.sigmoid(gate_a) * jnp.einsum("bsc,cd->bsd", y_a, w_branch[0])
              + jax.nn.sigmoid(gate_b) * jnp.einsum("bsc,cd->bsd", y_b, w_branch[1]))
    out = jnp.einsum("bsd,de->bse", merged, w_out)

    r = (DEEPNORM_ALPHA * x + out).astype(jnp.float32)
    mu = jnp.mean(r, axis=-1, keepdims=True)
    var = jnp.mean(jnp.square(r - mu), axis=-1, keepdims=True)
    y = (r - mu) * lax.rsqrt(var + LN_EPS) * ln_g.astype(jnp.float32) + ln_b.astype(jnp.float32)
    return y.astype(x.dtype)


def setup_inputs(seed: int = 0) -> dict:
    key = jax.random.key(seed)
    ks = jax.random.split(key, 11)
    x = jax.random.normal(ks[0], (BATCH, SEQ, D_MODEL), jnp.float32)
    seg_scale = (1.0, 1.0, DEEPNORM_BETA, 1.0, 1.0, 1.0, 1.0, 1.0, DEEPNORM_BETA, 1.0, 1.0, 1.0)
    col_scale = np.concatenate([np.full((n,), sc, np.float32) for n, sc in zip(IN_SIZES, seg_scale)])
    w_in = (jax.random.normal(ks[1], (DEPTH, D_MODEL, IN_COLS), jnp.float32)
            * (D_MODEL ** -0.5) * jnp.asarray(col_scale, jnp.float32))
    conv_w = jax.random.normal(ks[2], (DEPTH, CONV_WIDTH, 3 * DN_WIDTH), jnp.float32) * (CONV_WIDTH ** -0.5)
    a_log = jnp.log(jax.random.uniform(ks[3], (DEPTH, DN_HEADS), jnp.float32, minval=1.0, maxval=16.0))
    dt = jnp.exp(jax.random.uniform(ks[4], (DEPTH, DN_HEADS), jnp.float32,
                                    minval=float(np.log(1e-3)), maxval=float(np.log(1e-1))))
    dt_bias = dt + jnp.log(-jnp.expm1(-dt))
    dn_norm_w = 1.0 + 0.02 * jax.random.normal(ks[5], (DEPTH, DN_HEAD_DIM), jnp.float32)
    sinks = 0.5 * jax.random.normal(ks[6], (DEPTH, SWA_Q_HEADS), jnp.float32)
    w_branch = (jax.random.normal(ks[7], (DEPTH, N_BRANCH, BRANCH_WIDTH, D_MODEL), jnp.float32)
                * (BRANCH_WIDTH ** -0.5) * DEEPNORM_BETA)
    w_out = jax.random.normal(ks[8], (DEPTH, D_MODEL, D_MODEL), jnp.float32) * (D_MODEL ** -0.5) * DEEPNORM_BETA
    ln_g = 1.0 + 0.02 * jax.random.normal(ks[9], (DEPTH, D_MODEL), jnp.float32)
    ln_b = 0.02 * jax.random.normal(ks[10], (DEPTH, D_MODEL), jnp.float32)
    return {"x": x, "w_in": w_in, "conv_w": conv_w, "a_log": a_log, "dt_bias": dt_bias,
            "dn_norm_w": dn_norm_w, "sinks": sinks, "w_branch": w_branch, "w_out": w_out,
            "ln_g": ln_g, "ln_b": ln_b}


def reference(x, w_in, conv_w, a_log, dt_bias, dn_norm_w, sinks, w_branch, w_out, ln_g, ln_b):
    cos, sin = rope_tables(x.shape[1])
    for layer in range(DEPTH):
        x = hybrid_layer(x, w_in[layer], conv_w[layer], a_log[layer], dt_bias[layer], dn_norm_w[layer],
                         sinks[layer], w_branch[layer], w_out[layer], ln_g[layer], ln_b[layer], cos, sin)
    return x
```

```python
import numpy as np
from contextlib import ExitStack
import concourse.bass as bass
import concourse.mybir as mybir
from concourse.bass_utils import run_bass_kernel_spmd

F32 = mybir.dt.float32
BF16 = mybir.dt.bfloat16
AF = mybir.ActivationFunctionType
ALU = mybir.AluOpType
AX = mybir.AxisListType

D = 1024
NEG = -30000.0
ALPHA = 2.0 ** 0.25
LN_EPS = 1e-5
NORM_EPS = 1e-6
EPOCH = 8000
STOP = None
CROSS = True
EARLY_W = 3

O_DQ, O_DK, O_DV, O_DZ, O_DB, O_DA, O_SQ, O_SK, O_SV, O_SZ, O_GA, O_GB = (
    0, 512, 1024, 1536, 2048, 2052, 2056, 2568, 2696, 2824, 3336, 4360)
NF = 4608
NTM = 776

C_ID, C_ONE, C_TRI, C_SOWN, C_SA0, C_SA1 = 0, 128, 256, 384, 512, 640
NC32 = 768
K_ID, K_ONE, K_MS, K_MST, K_MU, K_MC, K_MP, K_MP0 = 0, 128, 192, 320, 448, 576, 704, 832
NC16 = 960


class Buf:
    def __init__(self, name, excl=False, parent=None):
        self.name = name
        self.w = None
        self.r = {}
        self.excl = excl
        self.parent = parent
        self.children = []
        if parent is not None:
            parent.children.append(self)


class Sched:
    ENGS = ("pe", "act", "dve", "pool", "sp")

    def __init__(self, nc, es):
        self.nc = nc
        self.es = es
        self.sem = {}
        self.cnt = {e: 0 for e in self.ENGS}
        self.prog = {e: [] for e in self.ENGS}
        self.waited = {e: {} for e in self.ENGS}
        self.nops = 0

    def _sem(self, key):
        if key not in self.sem:
            self.sem[key] = self.es.enter_context(self.nc.semaphore("s_%s_%s" % key))
        return self.sem[key]

    def dma_slot(self, name):
        self.cnt[name] = 0
        return name

    def op(self, e, fn, reads=(), writes=(), slot=None):
        deps = {}

        def need(tok):
            if tok is not None:
                k, v = tok
                if e == "pe" and k[0] == "pe":
                    return
                if deps.get(k, 0) < v:
                    deps[k] = v

        for b in reads:
            need(b.w)
            if b.excl:
                for k, v in b.r.items():
                    if k[0] != e:
                        need((k, v))
            if b.parent is not None:
                need(b.parent.w)
            for c in b.children:
                need(c.w)
        for b in writes:
            need(b.w)
            for k, v in b.r.items():
                need((k, v))
            if b.parent is not None:
                need(b.parent.w)
                for k, v in b.parent.r.items():
                    need((k, v))
            for c in b.children:
                need(c.w)
                for k, v in c.r.items():
                    need((k, v))
        if slot is not None and self.cnt[slot] > 0:
            k0 = (slot, 0)
            if deps.get(k0, 0) < self.cnt[slot]:
                deps[k0] = self.cnt[slot]
        for k, v in deps.items():
            if self.waited[e].get(k, 0) < v:
                self.prog[e].append(("wait", k, v))
                self.waited[e][k] = v
        if slot is None:
            c = self.cnt[e]
            key = (e, c // EPOCH)
            val = c % EPOCH + 1
            self.cnt[e] = c + 1
            inc = 1
        else:
            self.cnt[slot] += 16
            key = (slot, 0)
            val = self.cnt[slot]
            inc = 16
        self._sem(key)
        tok = (key, val)
        self.prog[e].append(("op", fn, key, inc))
        for b in writes:
            b.w = tok
            b.r = {}
        for b in reads:
            if b.r.get(key, 0) < val:
                b.r[key] = val
        self.nops += 1
        return tok

    def final_wait(self, e, bufs):
        for b in bufs:
            if b.w is not None:
                k, v = b.w
                self.prog[e].append(("wait", k, v))

    def emit(self):
        nc = self.nc
        sched = self
        with nc.Block() as block:
            def run(engname, eng):
                for item in sched.prog[engname]:
                    if item[0] == "wait":
                        eng.wait_ge(sched.sem[item[1]], item[2])
                    else:
                        _, fn, k, inc = item
                        fn(eng).then_inc(sched.sem[k], inc)

            @block.tensor
            def _(eng):
                run("pe", eng)

            @block.scalar
            def _(eng):
                run("act", eng)

            @block.vector
            def _(eng):
                run("dve", eng)

            @block.gpsimd
            def _(eng):
                run("pool", eng)

            @block.sync
            def _(eng):
                run("sp", eng)


def build(NPRE, NMAIN):
    NT = NPRE + NMAIN
    NSW = NMAIN + 1
    nc = bass.Bass("TRN2", target_bir_lowering=False)

    def din(name, shape):
        return nc.dram_tensor(name, list(shape), F32, kind="ExternalInput").ap()

    xT_d = din("xT", [NT, 128, 1024])
    xr_d = din("xr", [NMAIN, 128, 1024])
    wf_d = din("wf", [128, 8 * NF])
    wt_d = din("wt", [128, 8 * NTM])
    wb_d = din("wb", [128, 8192])
    wo_d = din("wo", [128, 8192])
    c32_d = din("c32", [128, NC32])
    c16_d = din("c16", [128, NC16])
    convw_d = din("convw", [128, 48])
    small_d = din("small", [128, 12])
    normw_d = din("normw", [1, 128])
    lng_d = din("lng", [1, 1024])
    lnb_d = din("lnb", [1, 1024])
    rope_d = din("rope", [128, NSW * 32])
    out_d = nc.dram_tensor("out", [NMAIN, 128, 1024], F32, kind="ExternalOutput").ap()

    with ExitStack() as es:
        def sb(name, shape, dt):
            return es.enter_context(nc.sbuf_tensor("sb_" + name, list(shape), dt))

        def ps(name, shape, dt):
            return es.enter_context(nc.psum_tensor("pp_" + name, list(shape), dt))

        S = Sched(nc, es)
        B = {}

        def buf(name, excl=False, parent=None):
            B[name] = Buf(name, excl, parent)
            return B[name]

        Wf = sb("Wf", [128, 8, NF], BF16); bWf = buf("Wf")
        Wt = sb("Wt", [128, 8, NTM], BF16); bWt = buf("Wt")
        Wb = sb("Wb", [128, 2, 4, 1024], BF16); bWb = buf("Wb")
        Wo = sb("Wo", [128, 8, 1024], BF16); bWo = buf("Wo")
        c32 = sb("c32", [128, NC32], F32); bc32 = buf("c32")
        c16 = sb("c16", [128, NC16], BF16); bc16 = buf("c16")
        convw = sb("convw", [128, 12, 4], F32); bconvw = buf("convw")
        small = sb("small", [128, 12], F32); bsmall = buf("small")
        cvec = sb("cvec", [128, 12], F32); bcvec = buf("cvec")
        normw = sb("normw", [128, 128], F32); bnormw = buf("normw")
        lng = sb("lng", [128, 1024], F32); blng = buf("lng")
        lnb = sb("lnb", [128, 1024], F32); blnb = buf("lnb")
        ropeT2 = sb("ropeT2", [128, 2, 32], F32)
        PTb = sb("PTb", [128, 2, 8, 128], BF16)
        tokg = sb("tokg", [128, 2, 8], F32)
        S32 = sb("S32", [128, 4, 128], F32); bS32 = buf("S32")
        S16 = sb("S16", [128, 4, 128], BF16); bS16 = buf("S16")
        halo = sb("halo", [128, 12, 3], F32); bhalo = buf("halo")
        kTr = sb("kTr", [128, 2, 128], BF16); bkTr = [buf("kTr0"), buf("kTr1")]
        vtok = sb("vtok", [128, 2, 128], BF16); bvtok = [buf("vtok0"), buf("vtok1")]
        slotA = sb("slotA", [128, 1572], F32); bA = buf("slotA")
        slotB = sb("slotB", [128, 1536], F32); bB = buf("slotB")
        slotC = sb("slotC", [128, 1536], F32); bC = buf("slotC")
        xr = sb("xr", [128, 1024], F32); bxr = buf("xr")
        xfb = sb("xfb", [128, 1024], F32); bxf = buf("xfb")
        xb2 = sb("xb2", [128, 2, 8, 128], BF16); bxb2 = [buf("xb_0"), buf("xb_1")]
        qkv = sb("qkv", [128, 12, 128], BF16); bqkv = buf("qkv")
        zA = sb("zA", [128, 4, 128], BF16); bzA = buf("zA")
        zB = sb("zB", [128, 4, 128], BF16); bzB = buf("zB")
        gt = sb("gt", [128, 16, 128], BF16); bgt = buf("gt")
        swq = sb("swq", [128, 512], BF16); bswq = buf("swq")
        swk = sb("swk", [128, 128], BF16); bswk = buf("swk")
        qT4 = sb("qT4", [128, 4, 128], BF16); bqT4 = buf("qT4")
        rtmp = sb("rtmp", [128, 2, 10, 16], F32); brtmp = buf("rtmp")
        tokd = sb("tokd", [128, 2, 96], F32); btokd = [buf("tok0"), buf("tok1")]
        vecsd = sb("vecsd", [128, 2, 3, 4], F32); bvecsd = [buf("vecs0"), buf("vecs1")]
        cvt = sb("cvt", [128, 4, 128], F32); bcvt = buf("cvt")
        intraT = sb("intraT", [128, 4, 128], BF16); bintra = buf("intraT")
        qdT = sb("qdT", [128, 4, 128], BF16); bqd = buf("qdT")
        TT = sb("TT", [128, 4, 128], BF16); bTT = buf("TT")
        kbg = sb("kbg", [128, 4, 128], BF16); bkbg = buf("kbg")
        kt = sb("kt", [128, 4, 128], BF16); bkt = buf("kt")
        vb = sb("vb", [128, 4, 128], BF16); bvb = buf("vb")
        wT = sb("wT", [128, 4, 128], BF16); bwT = buf("wT")
        u32 = sb("u32", [128, 4, 128], F32); bu = buf("u32")
        vnew = sb("vnew", [128, 4, 128], BF16); bvnew = buf("vnew")
        o32 = sb("o32", [128, 4, 128], F32); bo = buf("o32")
        on16 = sb("on16", [128, 4, 128], BF16); bon = buf("on16")
        yaT = sb("yaT", [128, 4, 128], BF16); bya = buf("yaT")
        ybT = sb("ybT", [128, 4, 128], BF16); byb = buf("ybT")
        mg = sb("mg", [128, 8, 128], BF16); bmg = buf("mg")
        r32 = sb("r32", [128, 1024], F32); br = buf("r32")
        br_lo = buf("r32lo", parent=br); br_hi = buf("r32hi", parent=br)
        st = sb("st", [128, 16], F32); bst = buf("st")

        hq = slotA[:, 0:1572].rearrange("p (c t) -> p c t", c=12)
        D3 = slotA[:, 0:1536].rearrange("p (k h t) -> p k h t", k=3, h=4)
        PT = slotA[:, 0:1024].bitcast(BF16).rearrange("p (b h q) -> p b h q", b=2, h=8)
        acc = slotB[:, 0:1536].rearrange("p (c t) -> p c t", c=12)
        Ees = slotB[:, 0:1024].bitcast(BF16).rearrange("p (k h t) -> p k h t", k=4, h=4)
        sq16 = slotB[:, 1024:1536].bitcast(BF16).rearrange("p (c t) -> p c t", c=8)
        xf = xfb[:, :].rearrange("p (c t) -> p c t", c=8)
        chX = slotC[:, 0:512].bitcast(BF16).rearrange("p (s h t) -> p s h t", s=2, h=4)
        chYQ = slotC[:, 512:1536].bitcast(BF16).rearrange("p (s h w t) -> p s h w t", s=2, h=4, w=2)
        tmpa = slotC[:, 0:512].rearrange("p (h t) -> p h t", h=4)
        tmpb = slotC[:, 512:1024].rearrange("p (h t) -> p h t", h=4)
        tmpc = slotC[:, 1024:1536].rearrange("p (h t) -> p h t", h=4)
        cvtp = r32[:, 512:1024].rearrange("p (h t) -> p h t", h=4)

        PS = [ps("ps%d" % i, [128, 512], F32) for i in range(8)]
        bPS = [buf("ps%d" % i, True) for i in range(8)]

        def P4(i):
            return PS[i][:, :].rearrange("p (h t) -> p h t", h=4)

        ld = S.dma_slot("ld")
        ldx = S.dma_slot("ldx")
        ldr = S.dma_slot("ldr")
        stq = S.dma_slot("st")

        id32 = c32[:, C_ID:C_ID + 128]
        ones32 = c32[:, C_ONE:C_ONE + 128]
        id16 = c16[:, K_ID:K_ID + 128]
        ones16 = c16[:, K_ONE:K_ONE + 64]

        S.op("sp", lambda e: e.dma_start(out=c32[:, :], in_=c32_d), writes=[bc32], slot=S.dma_slot("ldc1"))
        S.op("sp", lambda e: e.dma_start(out=r32[:, 0:NC16], in_=c16_d), writes=[br], slot=S.dma_slot("ldc2"))
        S.op("dve", lambda e: e.tensor_copy(out=c16[:, :], in_=r32[:, 0:NC16]), reads=[br], writes=[bc16])
        S.op("sp", lambda e: e.dma_start(out=convw[:, :, :].rearrange("p c j -> p (c j)"), in_=convw_d), writes=[bconvw], slot=S.dma_slot("ldc3"))
        S.op("sp", lambda e: e.dma_start(out=small[:, :], in_=small_d), writes=[bsmall], slot=S.dma_slot("ldc4"))
        S.op("sp", lambda e: e.dma_start(out=normw[:, :], in_=normw_d.partition_broadcast(128)), writes=[bnormw], slot=S.dma_slot("ldc5"))
        S.op("sp", lambda e: e.dma_start(out=lng[:, :], in_=lng_d.partition_broadcast(128)), writes=[blng], slot=S.dma_slot("ldc6"))
        S.op("sp", lambda e: e.dma_start(out=lnb[:, :], in_=lnb_d.partition_broadcast(128)), writes=[blnb], slot=S.dma_slot("ldc7"))
        S.op("act", lambda e: e.activation(out=cvec[:, 0:4], in_=small[:, 0:4], func=AF.Exp), reads=[bsmall], writes=[bcvec])
        S.op("dve", lambda e: e.tensor_scalar_mul(out=cvec[:, 0:4], in0=cvec[:, 0:4], scalar1=-1.0), reads=[bcvec], writes=[bcvec])
        S.op("dve", lambda e: e.tensor_copy(out=cvec[:, 4:8], in_=small[:, 4:8]), reads=[bsmall], writes=[bcvec])
        S.op("act", lambda e: e.activation(out=cvec[:, 8:12], in_=small[:, 8:12], func=AF.Exp), reads=[bsmall], writes=[bcvec])
        S.op("pool", lambda e: e.memset(halo[:, :, :], 0.0), writes=[bhalo])
        S.op("pool", lambda e: e.memset(kTr[:, :, :], 0.0), writes=bkTr)
        S.op("pool", lambda e: e.memset(vtok[:, :, :], 0.0), writes=bvtok)

        stg = [(xr[:, :], bxr), (r32[:, :], br), (slotC[:, 0:1024], bC)]
        stg_slots = [S.dma_slot("stg%d" % i) for i in range(3)]
        casters = ["dve", "pool", "act"]
        pieces = []
        Wf_flat = Wf[:, :, :].rearrange("p k c -> p (k c)")
        Wt_flat = Wt[:, :, :].rearrange("p k c -> p (k c)")
        Wb_flat = Wb[:, :, :, :].rearrange("p a b c -> p (a b c)")
        Wo_flat = Wo[:, :, :].rearrange("p k c -> p (k c)")
        for (src, dst, bdst, total) in ((wf_d, Wf_flat, bWf, 8 * NF), (wt_d, Wt_flat, bWt, 8 * NTM),
                                        (wb_d, Wb_flat, bWb, 8192), (wo_d, Wo_flat, bWo, 8192)):
            o = 0
            while o < total:
                w = min(1024, total - o)
                pieces.append((src, dst, bdst, o, w))
                o += w
        for i, (src, dst, bdst, o, w) in enumerate(pieces):
            sap, sbuf_ = stg[i % 3]
            ce = casters[i % 3]
            S.op("sp", lambda e, sap=sap, src=src, o=o, w=w: e.dma_start(out=sap[:, 0:w], in_=src[:, o:o + w]),
                 writes=[sbuf_], slot=stg_slots[i % 3])
            if ce == "act":
                S.op("act", lambda e, sap=sap, dst=dst, o=o, w=w: e.copy(out=dst[:, o:o + w], in_=sap[:, 0:w]),
                     reads=[sbuf_], writes=[bdst])
            else:
                S.op(ce, lambda e, sap=sap, dst=dst, o=o, w=w: e.tensor_copy(out=dst[:, o:o + w], in_=sap[:, 0:w]),
                     reads=[sbuf_], writes=[bdst])

        def mm(out, lhsT, rhs, start, stop, reads, writes, tp=None):
            if tp is None:
                S.op("pe", lambda e: e.matmul(out, lhsT=lhsT, rhs=rhs, start=start, stop=stop), reads=reads, writes=writes)
            else:
                S.op("pe", lambda e: e.matmul(out, lhsT=lhsT, rhs=rhs, start=start, stop=stop, tile_position=tp),
                     reads=reads, writes=writes)

        def tr(out, in_, reads, writes):
            S.op("pe", lambda e: e.transpose(out, in_, id16), reads=list(reads) + [bc16], writes=writes)

        def act(out, in_, func, reads, writes, bias=None, scale=None, accum=None):
            kw = {}
            if bias is not None:
                kw["bias"] = bias
            if scale is not None:
                kw["scale"] = scale
            if accum is not None:
                kw["accum_out"] = accum
            S.op("act", lambda e: e.activation(out=out, in_=in_, func=func, **kw), reads=reads, writes=writes)

        def tt(eng, out, in0, in1, op, reads, writes):
            S.op(eng, lambda e: e.tensor_tensor(out=out, in0=in0, in1=in1, op=op), reads=reads, writes=writes)

        def stt(out, in0, scalar, in1, op0, op1, reads, writes):
            S.op("dve", lambda e: e.scalar_tensor_tensor(out=out, in0=in0, scalar=scalar, in1=in1, op0=op0, op1=op1),
                 reads=reads, writes=writes)

        def bc_last(ap2, n):
            return ap2.unsqueeze(2).to_broadcast([128, ap2.shape[1], n])

        def bc_mid(ap2, k):
            return ap2.unsqueeze(1).to_broadcast([128, k, ap2.shape[1]])

        T_BETA, T_XA, T_EX, T_SP, T_G, T_LNB, T_LNK, T_LNQ, T_EA, T_EKT, T_EGL, T_TMP, T_MS, T_RSTD, T_GC = (
            0, 4, 8, 12, 16, 20, 24, 28, 32, 36, 40, 48, 52, 56, 60)


        def PBh(i):
            return PS[i][:, :].bitcast(BF16).rearrange("p (c t) -> p c t", c=8)

        GN = ("Ees", "chX", "chY", "chQ", "intra", "qd", "TT", "kbg", "kt", "vb", "wT", "u", "vnew", "o", "on", "ya",
              "S32", "S16", "tokg")
        baccK = buf("accK")
        baccV = buf("accV")
        bacc = [bmg, baccK, baccV]
        accv = [mg[:, :, :].rearrange("p c t -> p (c t)").bitcast(F32).rearrange("p (c t) -> p c t", c=4),
                u32[:, :, :], o32[:, :, :]]
        gpar = {"Ees": bB, "chX": bC, "chY": bC, "chQ": bC, "u": baccK, "o": baccV}
        gB = [{nm: buf("%s_g%d" % (nm, g), parent=gpar.get(nm)) for nm in GN} for g in range(2)]
        bsq = buf("sq16c", parent=bB)
        bPTb = buf("PTbuf")
        bropeT2 = [buf("ropeT_0"), buf("ropeT_1")]
        ldrope = S.dma_slot("ldrope")
        S.op("pool", lambda e: e.memset(S32[:, :, :], 0.0), writes=[gB[0]["S32"], gB[1]["S32"]])
        S.op("pool", lambda e: e.memset(S16[:, :, :], 0.0), writes=[gB[0]["S16"], gB[1]["S16"]])
        flags = {}

        def tile_info(n):
            is_main = n >= NPRE
            last_pre = (n == NPRE - 1)
            return dict(is_main=is_main, m=n - NPRE, last_pre=last_pre, need_sw=is_main or last_pre,
                        need_qh=is_main or last_pre, ti=n - (NPRE - 1), cur=n % 2, prv=1 - n % 2,
                        ch0=0 if is_main else 4)

        def rope_ops(src, dst, nh, rt, bsrc, bdst, tcol, brt):
            cc = rt[:, 0:16]
            ss_a = rt[:, 16:24]
            ss_b = rt[:, 24:32]
            t1 = rtmp[:, 0, tcol:tcol + nh, :]
            t2 = rtmp[:, 1, tcol:tcol + nh, :]
            S.op("dve", lambda e: e.tensor_copy(out=dst[:, :, 16:64], in_=src[:, :, 16:64]), reads=[bsrc], writes=[bdst])
            tt("dve", t1, src[:, :, 0:16], bc_mid(cc, nh), ALU.mult, [bsrc, brt], [brtmp])
            tt("dve", t2[:, :, 0:8], src[:, :, 8:16], bc_mid(ss_a, nh), ALU.mult, [bsrc, brt], [brtmp])
            tt("dve", t2[:, :, 8:16], src[:, :, 0:8], bc_mid(ss_b, nh), ALU.mult, [bsrc, brt], [brtmp])
            tt("dve", dst[:, :, 0:16], t1, t2, ALU.add, [brtmp], [bdst])

        def wait_flags(keys):
            while not all(flags.get(k) for k in keys):
                yield

        def early(n, streams=None):
            I = tile_info(n)
            is_main, need_sw, need_qh, ch0, cur, last_pre = I["is_main"], I["need_sw"], I["need_qh"], I["ch0"], I["cur"], I["last_pre"]
            tok = tokd[:, n % 2, :]
            vecs = vecsd[:, n % 2]
            xb = xb2[:, n % 2]
            bxb = bxb2[n % 2]

            def tk(c, w=4):
                return tok[:, c:c + w]
            pn = n - 1
            if need_sw:
                ti = I["ti"]
                S.op("sp", lambda e: e.dma_start(out=ropeT2[:, n % 2, :], in_=rope_d[:, ti * 32:(ti + 1) * 32]),
                     writes=[bropeT2[n % 2]], slot=ldrope)
            for kc in range(8):
                mm(PS[7][:, 256:264], xb[:, kc, :], Wt[:, kc, 768:776], kc == 0, kc == 7, [bxb, bWt], [bPS[7]])
            act(tk(T_TMP), PS[7][:, 256:260], AF.Exp, [bPS[7]], [btokd[n % 2]], scale=-1.0)
            tt("dve", tk(T_XA), PS[7][:, 260:264], cvec[:, 4:8], ALU.add, [bPS[7], bcvec], [btokd[n % 2]])
            act(tk(T_LNB), tk(T_TMP), AF.Ln, [btokd[n % 2]], [btokd[n % 2]], bias=1.0)
            act(tk(T_BETA), tk(T_LNB), AF.Exp, [btokd[n % 2]], [btokd[n % 2]], scale=-1.0)
            yield
            if pn >= 0:
                yield from wait_flags([("d3", pn, 0), ("d3", pn, 1)])
            S.op("pool", lambda e: e.tensor_copy(out=hq[:, ch0:12, 0:3], in_=halo[:, ch0:12, :]), reads=[bhalo], writes=[bA])
            qg = ([0] if need_qh else []) + [1, 2]
            ci = 0
            for g in qg:
                for j in range(4):
                    pb = 6 + ci % 2
                    col = (g * 4 + j) * 128
                    for kc in range(8):
                        mm(PS[pb][:, 0:128], Wf[:, kc, col:col + 128], xb[:, kc, :], kc == 0, kc == 7, [bWf, bxb], [bPS[pb]])
                    if ci % 2 == 0:
                        S.op("act", lambda e, c=g * 4 + j, pb=pb: e.copy(out=hq[:, c, 3:131], in_=PS[pb][:, 0:128]),
                             reads=[bPS[pb]], writes=[bA])
                    else:
                        S.op("dve", lambda e, c=g * 4 + j, pb=pb: e.tensor_copy(out=hq[:, c, 3:131], in_=PS[pb][:, 0:128]),
                             reads=[bPS[pb]], writes=[bA])
                    ci += 1
                    yield
            hc0 = 0 if need_qh else 4
            S.op("pool", lambda e: e.tensor_copy(out=halo[:, hc0:12, :], in_=hq[:, hc0:12, 128:131]), reads=[bA], writes=[bhalo])
            flags[("fm", n)] = True
            if not is_main:
                if last_pre:
                    for kc in range(8):
                        mm(PS[7][:, 0:256], xb[:, kc, :], Wt[:, kc, 512:768], kc == 0, kc == 7, [bxb, bWt], [bPS[7]])
                    srck = PS[7][:, 0:128].rearrange("p (g d) -> p g d", g=2)
                    dstk = swk[:, :].rearrange("p (g d) -> p g d", g=2)
                    rope_ops(srck, dstk, 2, ropeT2[:, n % 2, :], bPS[7], bswk, 8, bropeT2[n % 2])
                    S.op("act", lambda e: e.copy(out=vtok[:, cur, :], in_=PS[7][:, 128:256]), reads=[bPS[7]], writes=[bvtok[cur]])
                    tr(PBh(6)[:, 4, :], swk[:, :], [bswk], [bPS[6]])
                    S.op("act", lambda e: e.copy(out=kTr[:, cur, :], in_=PBh(6)[:, 4, :]), reads=[bPS[6]], writes=[bkTr[cur]])
            yield
            cgs = ([0] if is_main else []) + [1, 2]
            for j in range(4):
                for cg in cgs:
                    eng = "pool" if cg == 2 else "dve"
                    ct = cvtp if cg == 2 else cvt[:, :, :]
                    bct = br_hi if cg == 2 else bcvt
                    cs = slice(cg * 4, cg * 4 + 4)
                    wj = convw[:, cs, j:j + 1].to_broadcast([128, 4, 128])
                    if j == 0:
                        tt(eng, accv[cg], hq[:, cs, 0:128], wj, ALU.mult, [bA, bconvw], [bacc[cg]])
                    else:
                        tt(eng, ct, hq[:, cs, j:j + 128], wj, ALU.mult, [bA, bconvw], [bct])
                        if eng == "dve":
                            yield
                        tt(eng, accv[cg], accv[cg], ct, ALU.add, [bacc[cg], bct], [bacc[cg]])
                    if eng == "dve":
                        yield
            if pn >= 0:
                yield from wait_flags([("ph5", pn, 0), ("ph5", pn, 1), ("ees", pn, 0), ("ees", pn, 1)])
            for cg in cgs:
                act(qkv[:, cg * 4:cg * 4 + 4, :], accv[cg], AF.Silu, [bacc[cg]], [bqkv])
            flags[("silu", n)] = True
            if pn >= 0 and tile_info(pn)["is_main"]:
                yield from wait_flags([("lf", pn)])
                zgate_act(pn)
            yield
            sq0 = 0 if is_main else 4
            tt("dve", sq16[:, sq0:8, :], qkv[:, sq0:8, :], qkv[:, sq0:8, :], ALU.mult, [bqkv], [bsq])
            for c in range(sq0, 8):
                mm(PS[7][:, c:c + 1], sq16[:, c, :], ones16[:, 0:1], True, True, [bsq, bc16], [bPS[7]])
            bt = btokd[n % 2]
            bv = bvecsd[n % 2]
            act(tk(T_EX), tk(T_XA), AF.Exp, [bt], [bt])
            act(tk(T_LNK), PS[7][:, 4:8], AF.Ln, [bPS[7]], [bt], bias=NORM_EPS)
            if is_main:
                act(tk(T_LNQ), PS[7][:, 0:4], AF.Ln, [bPS[7]], [bt], bias=128.0 * NORM_EPS, scale=128.0)
            yield
            act(tk(T_SP), tk(T_EX), AF.Ln, [bt], [bt], bias=1.0)
            yield
            tt("dve", tk(T_G), tk(T_SP), cvec[:, 0:4], ALU.mult, [bt, bcvec], [bt])
            yield
            for i_, cm in enumerate((C_TRI, C_SOWN, C_SA0, C_SA1)):
                mm(PS[7][:, 384 + 4 * i_:388 + 4 * i_], c32[:, cm:cm + 128], tk(T_G), True, True, [bc32, bt], [bPS[7]])
            S.op("dve", lambda e: e.tensor_copy(out=tk(T_GC, 16), in_=PS[7][:, 384:400]), reads=[bPS[7]], writes=[bt])
            yield
            stt(tk(T_TMP), tk(T_LNK), -0.5, tk(T_LNB), ALU.mult, ALU.subtract, [bt], [bt])
            yield
            tt("dve", vecs[:, 1, :], tk(T_TMP), tk(T_GC), ALU.add, [bt], [bv])
            yield
            stt(vecs[:, 0, :], tk(T_LNK), -0.5, tk(T_GC), ALU.mult, ALU.subtract, [bt], [bv])
            if is_main:
                yield
                stt(vecs[:, 2, :], tk(T_LNQ), -0.5, tk(T_GC), ALU.mult, ALU.add, [bt], [bv])
            yield
            tt("dve", tk(T_TMP), tk(T_GC + 4), vecs[:, 0, :], ALU.add, [bt, bv], [bt])
            act(tk(T_EA), vecs[:, 1, :], AF.Exp, [bv], [bt])
            yield
            act(tk(T_EKT), tk(T_TMP), AF.Exp, [bt], [bt])
            act(tk(T_EGL, 8), tk(T_GC + 8, 8), AF.Exp, [bt], [bt])
            yield
            tt("dve", D3[:, 0], id32.unsqueeze(1).to_broadcast([128, 4, 128]),
               vecs[:, 0, :].unsqueeze(2).to_broadcast([128, 4, 128]), ALU.mult, [bc32, bv], [bA])
            if is_main:
                tt("dve", D3[:, 2], id32.unsqueeze(1).to_broadcast([128, 4, 128]),
                   vecs[:, 2, :].unsqueeze(2).to_broadcast([128, 4, 128]), ALU.mult, [bc32, bv], [bA])
            yield
            if streams is not None:
                streams.append(group(n, 0))
                streams.append(group(n, 1))

        def zgate_act(n):
            act(zA[:, :, :], zA[:, :, :], AF.Silu, [bzA], [bzA])
            act(zB[:, :, :], zB[:, :, :], AF.Silu, [bzB], [bzB])
            act(gt[:, :, :], gt[:, :, :], AF.Tanh, [bgt], [bgt], scale=0.5)
            flags[("zact", n)] = True

        def cast_tile(n):
            S.op("act", lambda e: e.copy(out=xb2[:, n % 2], in_=xf), reads=[bxf], writes=[bxb2[n % 2]])
            if n + 1 < NT:
                S.op("sp", lambda e: e.dma_start(out=xfb[:, :], in_=xT_d[n + 1]), writes=[bxf], slot=ldx)

        def late_front(n, streams=None):
            xb = xb2[:, n % 2]
            bxb = bxb2[n % 2]
            cur = n % 2
            S.op("sp", lambda e: e.dma_start(out=xr[:, :], in_=xr_d[n - NPRE]), writes=[bxr], slot=ldr)
            for kc in range(8):
                mm(PS[6][:, 0:512], xb[:, kc, :], Wt[:, kc, 0:512], kc == 0, kc == 7, [bxb, bWt], [bPS[6]])
            for kc in range(8):
                mm(PS[7][:, 0:256], xb[:, kc, :], Wt[:, kc, 512:768], kc == 0, kc == 7, [bxb, bWt], [bPS[7]])
            srcq = PS[6][:, 0:512].rearrange("p (s c d) -> p s c d", s=2, c=4)
            dstq = swq[:, :].rearrange("p (c s d) -> p s c d", c=4, s=2)
            for s_ in range(2):
                rope_ops(srcq[:, s_], dstq[:, s_], 4, ropeT2[:, n % 2, :], bPS[6], bswq, s_ * 4, bropeT2[n % 2])
            srck = PS[7][:, 0:128].rearrange("p (g d) -> p g d", g=2)
            dstk = swk[:, :].rearrange("p (g d) -> p g d", g=2)
            rope_ops(srck, dstk, 2, ropeT2[:, n % 2, :], bPS[7], bswk, 8, bropeT2[n % 2])
            S.op("act", lambda e: e.copy(out=vtok[:, cur, :], in_=PS[7][:, 128:256]), reads=[bPS[7]], writes=[bvtok[cur]])
            yield
            ci = 0
            for g in (3, 4, 5, 6, 7, 8):
                for j in range(4):
                    pb = 6 + ci % 2
                    col = (g * 4 + j) * 128
                    for kc in range(8):
                        mm(PS[pb][:, 0:128], Wf[:, kc, col:col + 128], xb[:, kc, :], kc == 0, kc == 7, [bWf, bxb], [bPS[pb]])
                    if g == 3:
                        dst, bd = zA[:, j, :], bzA
                    elif g == 4:
                        dst, bd = zB[:, j, :], bzB
                    else:
                        dst, bd = gt[:, (g - 5) * 4 + j, :], bgt
                    if ci % 2 == 0:
                        S.op("act", lambda e, dst=dst, pb=pb: e.copy(out=dst, in_=PS[pb][:, 0:128]), reads=[bPS[pb]], writes=[bd])
                    else:
                        S.op("dve", lambda e, dst=dst, pb=pb: e.tensor_copy(out=dst, in_=PS[pb][:, 0:128]), reads=[bPS[pb]], writes=[bd])
                    ci += 1
                    yield
            flags[("lf", n)] = True
            if n == NT - 1:
                zgate_act(n)
            if streams is not None:
                streams.append(wswa(n))

        def group(n, g):
            I = tile_info(n)
            is_main = I["is_main"]
            G = gB[g]

            def ev(kind, out, in_, reads, writes):
                if g == 0:
                    S.op("act", lambda e: e.copy(out=out, in_=in_), reads=reads, writes=writes)
                else:
                    S.op("dve", lambda e: e.tensor_copy(out=out, in_=in_), reads=reads, writes=writes)
            tok = tokd[:, n % 2, :]
            vecs = vecsd[:, n % 2]
            b0, b1, b2 = 3 * g, 3 * g + 1, 3 * g + 2
            hs = slice(2 * g, 2 * g + 2)
            EG = Ees[:, :, hs, :]
            pbh = PBh(b1)
            for hh in range(2):
                h = 2 * g + hh
                tr(pbh[:, hh, :], qkv[:, 4 + h, :], [bqkv], [bPS[b1]])
                tr(pbh[:, 2 + hh, :], qkv[:, 8 + h, :], [bqkv], [bPS[b1]])
            flags[("ph5", n, g)] = True
            yield
            tt("dve", kbg[:, hs, :], pbh[:, 0:2, :], bc_last(tok[:, T_EA + 2 * g:T_EA + 2 * g + 2], 128), ALU.mult,
               [bPS[b1], btokd[n % 2]], [G["kbg"]])
            tt("dve", vb[:, hs, :], pbh[:, 2:4, :], bc_last(tok[:, T_BETA + 2 * g:T_BETA + 2 * g + 2], 128), ALU.mult,
               [bPS[b1], btokd[n % 2]], [G["vb"]])
            tt("dve", kt[:, hs, :], pbh[:, 0:2, :], bc_last(tok[:, T_EKT + 2 * g:T_EKT + 2 * g + 2], 128), ALU.mult,
               [bPS[b1], btokd[n % 2]], [G["kt"]])
            yield
            specs = [(b0, 0, 0, K_MS, 1, 0)]
            if is_main:
                specs += [(b1, 0, 2, K_MU, 0, 2), (b1, 256, 2, None, None, 3)]
            for (bank, off, kind, mk, bias_kind, eidx) in specs:
                for hh in range(2):
                    h = 2 * g + hh
                    reg = PS[bank][:, off + hh * 128:off + (hh + 1) * 128]
                    mm(reg, ones32, D3[:, kind, h, :], True, mk is None, [bc32, bA], [bPS[bank]])
                    if mk is not None:
                        mm(reg, id16, c16[:, mk:mk + 128], False, True, [bc16], [bPS[bank]])
            for hh in range(2):
                h = 2 * g + hh
                mm(PS[b2][:, hh * 128:(hh + 1) * 128], qkv[:, 4 + h, :], qkv[:, 4 + h, :], True, True, [bqkv], [bPS[b2]])
            if is_main:
                for hh in range(2):
                    h = 2 * g + hh
                    mm(PS[b2][:, 256 + hh * 128:256 + (hh + 1) * 128], qkv[:, 4 + h, :], qkv[:, h, :], True, True,
                       [bqkv], [bPS[b2]])
            flags[("d3", n, g)] = True
            yield
            for (bank, off, kind, mk, bias_kind, eidx) in specs:
                if mk is not None:
                    for hh in range(2):
                        h = 2 * g + hh
                        act(EG[:, eidx, hh, :], PS[bank][:, off + hh * 128:off + (hh + 1) * 128], AF.Exp,
                            [bPS[bank], bvecsd[n % 2]], [G["Ees"]], bias=vecs[:, bias_kind, h:h + 1])
                else:
                    act(EG[:, eidx], PS[bank][:, off:off + 256].rearrange("p (h t) -> p h t", h=2), AF.Exp,
                        [bPS[bank]], [G["Ees"]])
                yield
            Graw = PS[b2][:, 0:256].rearrange("p (h t) -> p h t", h=2)
            stt(chX[:, 0, hs], Graw, -1.0, EG[:, 0], ALU.mult, ALU.mult, [bPS[b2], G["Ees"]], [G["chX"]])
            pbt = PBh(b0)
            for hh in range(2):
                tr(pbt[:, 4 + hh, :], chX[:, 0, 2 * g + hh, :], [G["chX"]], [bPS[b0]])
            ev("copy", chYQ[:, 0, hs, 0, :], pbt[:, 4:6, :], [bPS[b0]], [G["chY"]])
            if is_main:
                tt("dve", intraT[:, hs, :], PS[b2][:, 256:512].rearrange("p (h t) -> p h t", h=2), EG[:, 2], ALU.mult,
                   [bPS[b2], G["Ees"]], [G["intra"]])
            yield
            tt("dve", chYQ[:, 0, hs, 1, :], chYQ[:, 0, hs, 0, :], bc_mid(id16, 2), ALU.add, [G["chY"], bc16], [G["chQ"]])
            if is_main:
                tt("dve", qdT[:, hs, :], qkv[:, hs, :], EG[:, 3], ALU.mult, [bqkv, G["Ees"]], [G["qd"]])
            flags[("ees", n, g)] = True
            yield
            chb = [G["chX"], G["chY"], G["chQ"]]
            for k in range(6):
                s0 = k % 2
                s1 = 1 - s0
                bk = b0 if k % 2 == 0 else b1
                if k < 5:
                    for hh in range(2):
                        h = 2 * g + hh
                        off = hh * 256
                        mm(PS[bk][:, off:off + 128], chX[:, s0, h, :], chYQ[:, s0, h, 0, :], True, True, chb, [bPS[bk]])
                        if k > 0:
                            mm(PS[bk][:, off + 128:off + 256], chX[:, s0, h, :], chYQ[:, s0, h, 1, :], True, False, chb, [bPS[bk]])
                            mm(PS[bk][:, off + 128:off + 256], id16, chYQ[:, s0, h, 1, :], False, True, chb + [bc16], [bPS[bk]])
                        mm(PS[b2][:, hh * 128:(hh + 1) * 128], chYQ[:, s0, h, 0, :], chX[:, s0, h, :], True, True,
                           chb, [bPS[b2]])
                    yield
                    pv = PS[bk][:, :].rearrange("p (h w t) -> p h w t", h=2, w=2)
                    ev("copy", chX[:, s1, hs], PS[b2][:, 0:256].rearrange("p (h t) -> p h t", h=2), [bPS[b2]], [G["chX"]])
                    if k == 0:
                        S.op("dve", lambda e, s0=s0, s1=s1: e.tensor_copy(out=chYQ[:, s1, hs, 1, :], in_=chYQ[:, s0, hs, 1, :]),
                             reads=[G["chQ"]], writes=[G["chQ"]])
                        ev("copy", chYQ[:, s1, hs, 0, :], pv[:, :, 0, :], [bPS[bk]], [G["chY"]])
                    else:
                        ev("copy", chYQ[:, s1, hs, :, :], pv, [bPS[bk]], [G["chY"], G["chQ"]])
                    yield
                else:
                    for hh in range(2):
                        h = 2 * g + hh
                        mm(PS[bk][:, hh * 128:(hh + 1) * 128], chX[:, s0, h, :], chYQ[:, s0, h, 1, :], True, False, chb, [bPS[bk]])
                        mm(PS[bk][:, hh * 128:(hh + 1) * 128], id16, chYQ[:, s0, h, 1, :], False, True, chb + [bc16], [bPS[bk]])
                    yield
                    ev("copy", TT[:, hs, :], PS[bk][:, 0:256].rearrange("p (h t) -> p h t", h=2), [bPS[bk]], [G["TT"]])
                    yield
            if n + 1 < NT:
                yield from wait_flags([("silu", n + 1)])
            for hh in range(2):
                h = 2 * g + hh
                mm(PS[b1][:, hh * 128:(hh + 1) * 128], kbg[:, h, :], TT[:, h, :], True, True, [G["kbg"], G["TT"]], [bPS[b1]])
                mm(PS[b1][:, 256 + hh * 128:256 + (hh + 1) * 128], TT[:, h, :], vb[:, h, :], True, True, [G["TT"], G["vb"]],
                   [bPS[b1]])
            yield
            S.op("act", lambda e: e.copy(out=wT[:, hs, :], in_=PS[b1][:, 0:256].rearrange("p (h t) -> p h t", h=2)),
                 reads=[bPS[b1]], writes=[G["wT"]])
            S.op("act", lambda e: e.copy(out=u32[:, hs, :], in_=PS[b1][:, 256:512].rearrange("p (h t) -> p h t", h=2)),
                 reads=[bPS[b1]], writes=[G["u"]])
            yield
            for c in range(2):
                r0 = c * 64
                rs = slice(r0, r0 + 64)
                for hh in range(2):
                    h = 2 * g + hh
                    mm(PS[b2][rs, hh * 128:(hh + 1) * 128], wT[:, h, rs], S16[:, h, :], True, True, [G["wT"], G["S16"]],
                       [bPS[b2]], tp=(0, r0))
                yield
                tt("dve", vnew[rs, hs, :], u32[rs, hs, :], PS[b2][rs, 0:256].rearrange("p (h t) -> p h t", h=2), ALU.subtract,
                   [G["u"], bPS[b2]], [G["vnew"]])
                tt("dve", S32[:, hs, :], S32[:, hs, :], bc_last(tok[:, T_EGL + 4 * c + 2 * g:T_EGL + 4 * c + 2 * g + 2], 128),
                   ALU.mult, [G["S32"], btokd[n % 2]], [G["S32"]])
                yield
                if is_main:
                    for hh in range(2):
                        h = 2 * g + hh
                        mm(PS[b0][rs, hh * 128:(hh + 1) * 128], qdT[:, h, rs], S16[:, h, :], True, False, [G["qd"], G["S16"]],
                           [bPS[b0]], tp=(0, r0))
                        mm(PS[b0][rs, hh * 128:(hh + 1) * 128], intraT[rs, h, rs], vnew[rs, h, :], False, True,
                           [G["intra"], G["vnew"]], [bPS[b0]], tp=(r0, r0))
                for hh in range(2):
                    h = 2 * g + hh
                    mm(PS[b1][:, hh * 128:(hh + 1) * 128], kt[rs, h, :], vnew[rs, h, :], True, True, [G["kt"], G["vnew"]],
                       [bPS[b1]], tp=(r0, 0))
                yield
                tt("dve", S32[:, hs, :], S32[:, hs, :], PS[b1][:, 0:256].rearrange("p (h t) -> p h t", h=2), ALU.add,
                   [G["S32"], bPS[b1]], [G["S32"]])
                S.op("act", lambda e: e.copy(out=S16[:, hs, :], in_=S32[:, hs, :]), reads=[G["S32"]], writes=[G["S16"]])
                if is_main:
                    S.op("act", lambda e, rs=rs: e.copy(out=o32[rs, hs, :], in_=PS[b0][rs, 0:256].rearrange("p (h t) -> p h t", h=2)),
                         reads=[bPS[b0]], writes=[G["o"]])
                yield
            if not is_main:
                return
            tg = tokg[:, g, :]
            for hh in range(2):
                h = 2 * g + hh
                act(on16[:, h, :], o32[:, h, :], AF.Square, [G["o"]], [G["on"], G["tokg"]], accum=tg[:, hh:hh + 1])
            act(tg[:, 2:4], tg[:, 0:2], AF.Ln, [G["tokg"]], [G["tokg"]], bias=NORM_EPS, scale=1.0 / 128.0)
            act(tg[:, 2:4], tg[:, 2:4], AF.Exp, [G["tokg"]], [G["tokg"]], scale=-0.5)
            yield
            tt("dve", o32[:, hs, :], o32[:, hs, :], bc_last(tg[:, 2:4], 128), ALU.mult, [G["o"], G["tokg"]], [G["o"]])
            tt("dve", on16[:, hs, :], o32[:, hs, :], bc_mid(normw[:, :], 2), ALU.mult, [G["o"], bnormw], [G["on"]])
            yield
            yield
            yield from wait_flags([("zact", n)])
            pbh2 = PBh(b2)
            for hh in range(2):
                h = 2 * g + hh
                tr(pbh2[:, hh, :], on16[:, h, :], [G["on"]], [bPS[b2]])
            tt("dve", yaT[:, hs, :], pbh2[:, 0:2, :], zA[:, hs, :], ALU.mult, [bPS[b2], bzA], [G["ya"]])
            yield

        def wswa(n):
            I = tile_info(n)
            m, cur, prv = I["m"], I["cur"], I["prv"]
            p6 = PBh(6)
            for c in range(4):
                tr(p6[:, c, :], swq[:, c * 128:(c + 1) * 128], [bswq], [bPS[6]])
            tr(p6[:, 4, :], swk[:, :], [bswk], [bPS[6]])
            S.op("act", lambda e: e.copy(out=qT4[:, :, :], in_=p6[:, 0:4, :]), reads=[bPS[6]], writes=[bqT4])
            S.op("act", lambda e: e.copy(out=kTr[:, cur, :], in_=p6[:, 4, :]), reads=[bPS[6]], writes=[bkTr[cur]])
            yield
            for blk in range(2):
                kslot = prv if blk == 0 else cur
                if blk == 1:
                    mk = K_MC
                else:
                    mk = K_MP0 if m == 0 else K_MP
                for h in range(8):
                    s_, c_ = h // 4, h % 4
                    bank = 6 + s_
                    reg = PS[bank][:, c_ * 128:(c_ + 1) * 128]
                    mm(reg, kTr[s_ * 64:(s_ + 1) * 64, kslot, :], qT4[s_ * 64:(s_ + 1) * 64, c_, :], True, False,
                       [bkTr[kslot], bqT4], [bPS[bank]], tp=(s_ * 64, 0))
                    mm(reg, id16, c16[:, mk:mk + 128], False, True, [bc16], [bPS[bank]])
                for s_ in range(2):
                    act(PTb[:, blk, s_ * 4:(s_ + 1) * 4, :], P4(6 + s_), AF.Exp, [bPS[6 + s_]], [bPTb], scale=0.125)
                yield
            for h in range(8):
                g_ = h // 4
                po = (h % 2) * 64
                co = (h // 2) * 128
                for blk in range(2):
                    vslot = prv if blk == 0 else cur
                    mm(PS[6][po:po + 64, co:co + 128], vtok[:, vslot, g_ * 64:(g_ + 1) * 64], PTb[:, blk, h, :], blk == 0, blk == 1,
                       [bvtok[vslot], bPTb], [bPS[6]], tp=(0, po))
                for blk in range(2):
                    mm(PS[7][po:po + 64, co:co + 128], ones16, PTb[:, blk, h, :], blk == 0, blk == 1, [bc16, bPTb], [bPS[7]], tp=(0, po))
            wt_ = r32[:, 0:512].rearrange("p (h t) -> p h t", h=4)
            tt("dve", wt_, P4(7), bc_last(cvec[:, 8:12], 128), ALU.add, [bPS[7], bcvec], [br_lo])
            act(wt_, wt_, AF.Ln, [br_lo], [br_lo])
            act(wt_, wt_, AF.Exp, [br_lo], [br_lo], scale=-1.0)
            tt("dve", wt_, P4(6), wt_, ALU.mult, [bPS[6], br_lo], [br_lo])
            yield
            yield from wait_flags([("zact", n)])
            tt("pool", ybT[:, :, :], wt_, zB[:, :, :], ALU.mult, [br_lo, bzB], [byb])
            yield

        def join(n):
            I = tile_info(n)
            m = I["m"]
            yab = [gB[0]["ya"], gB[1]["ya"]]
            for br_, (src, bsrc, banks) in enumerate(((yaT, yab, (0, 1)), (ybT, [byb], (2, 3)))):
                for mo in range(8):
                    bank = banks[mo // 4]
                    for cc_ in range(4):
                        mm(PS[bank][:, (mo % 4) * 128:(mo % 4 + 1) * 128], Wb[:, br_, cc_, mo * 128:(mo + 1) * 128], src[:, cc_, :],
                           cc_ == 0, cc_ == 3, [bWb] + bsrc, [bPS[bank]])
            for half in range(2):
                stt(tmpa, gt[:, half * 4:half * 4 + 4, :], 1.0, P4(half), ALU.add, ALU.mult, [bPS[half], bgt], [bC])
                stt(tmpb, gt[:, 8 + half * 4:12 + half * 4, :], 1.0, P4(2 + half), ALU.add, ALU.mult, [bPS[2 + half], bgt], [bC])
                tt("pool", mg[:, half * 4:half * 4 + 4, :], tmpa, tmpb, ALU.add, [], [bmg, bC])
            for half in range(2):
                for kc in range(8):
                    mm(PS[4 + half][:, 0:512], mg[:, kc, :], Wo[:, kc, half * 512:(half + 1) * 512], kc == 0, kc == 7,
                       [bmg, bWo], [bPS[4 + half]])
            for half in range(2):
                stt(r32[:, half * 512:(half + 1) * 512], xr[:, half * 512:(half + 1) * 512], 2.0 * ALPHA, PS[4 + half][:, 0:512],
                    ALU.mult, ALU.add, [bxr, bPS[4 + half]], [br])
            S.op("dve", lambda e: e.reduce_sum(out=st[:, 0:1], in_=r32[:, :], axis=AX.X), reads=[br], writes=[bst])
            jk = tmpa.rearrange("p h t -> p (h t)")
            act(jk, r32[:, 0:512], AF.Square, [br], [bC, bst], accum=st[:, 1:2])
            act(jk, r32[:, 512:1024], AF.Square, [br], [bC, bst], accum=st[:, 2:3])
            S.op("dve", lambda e: e.tensor_scalar_mul(out=st[:, 3:4], in0=st[:, 0:1], scalar1=1.0 / 1024.0), reads=[bst], writes=[bst])
            tt("dve", st[:, 4:5], st[:, 1:2], st[:, 2:3], ALU.add, [bst], [bst])
            tt("dve", st[:, 5:6], st[:, 3:4], st[:, 3:4], ALU.mult, [bst], [bst])
            stt(st[:, 6:7], st[:, 4:5], 1.0 / 1024.0, st[:, 5:6], ALU.mult, ALU.subtract, [bst], [bst])
            act(st[:, 7:8], st[:, 6:7], AF.Ln, [bst], [bst], bias=4.0 * LN_EPS)
            act(st[:, 7:8], st[:, 7:8], AF.Exp, [bst], [bst], scale=-0.5)
            stt(st[:, 8:9], st[:, 3:4], -1.0, st[:, 7:8], ALU.mult, ALU.mult, [bst], [bst])
            act(r32[:, :], r32[:, :], AF.Identity, [br, bst], [br], bias=st[:, 8:9], scale=st[:, 7:8])
            tt("dve", r32[:, 0:512], r32[:, 0:512], lng[:, 0:512], ALU.mult, [br, blng], [br])
            tt("pool", r32[:, 512:1024], r32[:, 512:1024], lng[:, 512:1024], ALU.mult, [br, blng], [br])
            tt("dve", r32[:, 0:512], r32[:, 0:512], lnb[:, 0:512], ALU.add, [br, blnb], [br])
            tt("pool", r32[:, 512:1024], r32[:, 512:1024], lnb[:, 512:1024], ALU.add, [br, blnb], [br])
            S.op("sp", lambda e: e.dma_start(out=out_d[m], in_=r32[:, :]), reads=[br], writes=[buf("o%d" % m)], slot=stq)

        def run_streams(streams, weights=None):
            weights = weights or {}
            while streams:
                for gen in list(streams):
                    for _ in range(weights.get(id(gen), 1)):
                        try:
                            next(gen)
                        except StopIteration:
                            streams.remove(gen)
                            break

        S.op("sp", lambda e: e.dma_start(out=xfb[:, :], in_=xT_d[0]), writes=[bxf], slot=ldx)
        cast_tile(0)
        run_streams([early(0)])
        for n in range(NT):
            I = tile_info(n)
            if STOP is not None and STOP <= n:
                break
            if n + 1 < NT:
                cast_tile(n + 1)
            streams = [group(n, 0), group(n, 1)]
            if I["is_main"]:
                streams.append(late_front(n, streams))
            wts = {}
            if n + 1 < NT:
                eg = early(n + 1)
                streams.insert(0, eg)
                wts[id(eg)] = EARLY_W
            run_streams(streams, wts)
            if I["is_main"]:
                join(n)

        if STOP is not None:
            for m in range(NMAIN):
                S.op("sp", lambda e, m=m: e.dma_start(out=out_d[m], in_=lng[:, :]), reads=[blng], writes=[buf("o%d" % m)], slot=stq)
        S.final_wait("sp", [B["o%d" % m] for m in range(NMAIN)])
        S.emit()
    return nc


def _consts(first_half):
    i = np.arange(128)
    same = (i[:, None] // 64) == (i[None, :] // 64)
    c32 = np.zeros((128, NC32), np.float32)
    c32[:, C_ID:C_ID + 128] = np.eye(128, dtype=np.float32)
    c32[:, C_ONE:C_ONE + 128] = 1.0
    c32[:, C_TRI:C_TRI + 128] = ((i[:, None] <= i[None, :]) & same)
    c32[:, C_SOWN:C_SOWN + 128] = same
    c32[:, C_SA0:C_SA0 + 128] = (i[:, None] < 64)
    c32[:, C_SA1:C_SA1 + 128] = (i[:, None] >= 64)
    c16 = np.zeros((128, NC16), np.float32)
    c16[:, K_ID:K_ID + 128] = np.eye(128, dtype=np.float32)
    c16[:, K_ONE:K_ONE + 64] = 1.0
    p, f = i[:, None], i[None, :]
    c16[:, K_MS:K_MS + 128] = np.where((p > f) & same, 0.0, NEG)
    c16[:, K_MST:K_MST + 128] = np.where((f > p) & same, 0.0, NEG)
    c16[:, K_MU:K_MU + 128] = np.where((f >= p) & same, 0.0, NEG)
    c16[:, K_MC:K_MC + 128] = np.where(f >= p, 0.0, NEG)
    mp = np.where(p > f, 0.0, NEG)
    c16[:, K_MP:K_MP + 128] = mp
    c16[:, K_MP0:K_MP0 + 128] = NEG if first_half else mp
    return c32, c16


def _rope_table(pos0, nsw):
    inv_freq = (np.float32(500000.0) ** (-np.arange(0, 16, 2, dtype=np.float32) / np.float32(16))).astype(np.float32)
    pos = (pos0 + np.arange(nsw * 128)).astype(np.float32)
    ang = (pos[:, None] * inv_freq[None, :]).astype(np.float32)
    c, s = np.cos(ang).astype(np.float32), np.sin(ang).astype(np.float32)
    tab = np.concatenate([c, c, -s, s], axis=1).reshape(nsw, 128, 32)
    return np.ascontiguousarray(tab.transpose(1, 0, 2)).reshape(128, nsw * 32)


def _tile_w(w, cols):
    ws = w[:, cols]
    return np.ascontiguousarray(ws.reshape(8, 128, -1).transpose(1, 0, 2)).reshape(128, -1)


def prepare_inputs(x, w_in, conv_w, a_log, dt_bias, dn_norm_w, sinks, w_branch, w_out, ln_g, ln_b, npre, nmain):
    x = np.asarray(x, np.float32)
    nb, seq, _ = x.shape
    T = nmain * 128
    assert seq == 2 * T and npre == nmain
    w_in = np.asarray(w_in, np.float32)[0]
    colf = np.concatenate([np.arange(O_DQ, O_DZ + 512), np.arange(O_SZ, O_SZ + 512), np.arange(O_GA, O_GB + 1024)])
    colt = np.concatenate([np.arange(O_SQ, O_SQ + 512), np.arange(O_SK, O_SK + 128), np.arange(O_SV, O_SV + 128),
                           np.arange(O_DB, O_DB + 4), np.arange(O_DA, O_DA + 4)])
    assert colf.size == NF and colt.size == NTM
    wf = _tile_w(w_in, colf)
    wt = _tile_w(w_in, colt)
    wbr = np.asarray(w_branch, np.float32)[0]
    wb = np.ascontiguousarray(wbr.reshape(2, 4, 128, 1024).transpose(2, 0, 1, 3)).reshape(128, 8192)
    wo = np.ascontiguousarray(np.asarray(w_out, np.float32)[0].reshape(8, 128, 1024).transpose(1, 0, 2)).reshape(128, 8192)
    cw = np.asarray(conv_w, np.float32)[0]
    convw = np.ascontiguousarray(cw.reshape(4, 12, 128).transpose(2, 1, 0)).reshape(128, 48)
    small = np.zeros((128, 12), np.float32)
    small[:, 0:4] = np.asarray(a_log, np.float32)[0][None, :]
    small[:, 4:8] = np.asarray(dt_bias, np.float32)[0][None, :]
    sk = np.asarray(sinks, np.float32)[0]
    for c in range(4):
        small[0:64, 8 + c] = sk[2 * c]
        small[64:128, 8 + c] = sk[2 * c + 1]
    normw = np.asarray(dn_norm_w, np.float32)[0][None, :]
    lng = np.asarray(ln_g, np.float32)[0][None, :]
    lnb = np.asarray(ln_b, np.float32)[0][None, :]
    in_maps = []
    for b in range(nb):
        for h in range(2):
            start = h * T
            xa = np.zeros((npre * 128 + T, D), np.float32)
            if h == 1:
                xa[:] = x[b, 0:2 * T]
            else:
                xa[npre * 128:] = x[b, 0:T]
            nt = npre + nmain
            xT = np.ascontiguousarray(xa.reshape(nt, 128, 8, 128).transpose(0, 3, 2, 1)).reshape(nt, 128, 1024)
            xr = np.ascontiguousarray(x[b, start:start + T].reshape(nmain, 128, D))
            c32, c16 = _consts(h == 0)
            rope = _rope_table(start - 128, nmain + 1)
            in_maps.append({"xT": xT, "xr": xr, "wf": wf, "wt": wt, "wb": wb, "wo": wo, "c32": c32, "c16": c16,
                            "convw": convw, "small": small, "normw": normw, "lng": lng, "lnb": lnb, "rope": rope})
    return in_maps


_NC_CACHE = {}


def run(inputs, npre, nmain):
    in_maps = prepare_inputs(npre=npre, nmain=nmain, **inputs)
    key = (npre, nmain)
    if key not in _NC_CACHE:
        _NC_CACHE[key] = build(npre, nmain)
    nc = _NC_CACHE[key]
    ncores = len(in_maps)
    res = run_bass_kernel_spmd(nc, in_maps, core_ids=list(range(ncores)))
    nb = ncores // 2
    T = nmain * 128
    out = np.zeros((nb, 2 * T, D), np.float32)
    for b in range(nb):
        for h in range(2):
            out[b, h * T:(h + 1) * T] = res.results[b * 2 + h]["out"].reshape(T, D)
    return out


def kernel(x, w_in, conv_w, a_log, dt_bias, dn_norm_w, sinks, w_branch, w_out, ln_g, ln_b):
    inputs = dict(x=x, w_in=w_in, conv_w=conv_w, a_log=a_log, dt_bias=dt_bias, dn_norm_w=dn_norm_w, sinks=sinks,
                  w_branch=w_branch, w_out=w_out, ln_g=ln_g, ln_b=ln_b)
    return run(inputs, 32, 32)
```

```python
import numpy as np
from contextlib import ExitStack
import concourse.bass as bass
import concourse.mybir as mybir
from concourse.bass_utils import run_bass_kernel_spmd

F32 = mybir.dt.float32
BF16 = mybir.dt.bfloat16
AF = mybir.ActivationFunctionType
ALU = mybir.AluOpType
AX = mybir.AxisListType

D = 1024
NEG = -30000.0
ALPHA = 2.0 ** 0.25
LN_EPS = 1e-5
NORM_EPS = 1e-6
EPOCH = 8000
STOP = None
CROSS = True
EARLY_W = 1

O_DQ, O_DK, O_DV, O_DZ, O_DB, O_DA, O_SQ, O_SK, O_SV, O_SZ, O_GA, O_GB = (
    0, 512, 1024, 1536, 2048, 2052, 2056, 2568, 2696, 2824, 3336, 4360)
NF = 4608
NTM = 776

C_ID, C_ONE, C_TRI, C_SOWN, C_SA0, C_SA1 = 0, 128, 256, 384, 512, 640
NC32 = 768
K_ID, K_ONE, K_MS, K_MST, K_MU, K_MC, K_MP, K_MP0 = 0, 128, 192, 320, 448, 576, 704, 832
NC16 = 960


class Buf:
    def __init__(self, name, excl=False, parent=None):
        self.name = name
        self.w = None
        self.r = {}
        self.excl = excl
        self.parent = parent
        self.children = []
        if parent is not None:
            parent.children.append(self)


class Sched:
    ENGS = ("pe", "act", "dve", "pool", "sp")

    def __init__(self, nc, es):
        self.nc = nc
        self.es = es
        self.sem = {}
        self.cnt = {e: 0 for e in self.ENGS}
        self.prog = {e: [] for e in self.ENGS}
        self.waited = {e: {} for e in self.ENGS}
        self.nops = 0

    def _sem(self, key):
        if key not in self.sem:
            self.sem[key] = self.es.enter_context(self.nc.semaphore("s_%s_%s" % key))
        return self.sem[key]

    def dma_slot(self, name):
        self.cnt[name] = 0
        return name

    def op(self, e, fn, reads=(), writes=(), slot=None):
        deps = {}

        def need(tok):
            if tok is not None:
                k, v = tok
                if e == "pe" and k[0] == "pe":
                    return
                if deps.get(k, 0) < v:
                    deps[k] = v

        for b in reads:
            need(b.w)
            if b.excl:
                for k, v in b.r.items():
                    if k[0] != e:
                        need((k, v))
            if b.parent is not None:
                need(b.parent.w)
            for c in b.children:
                need(c.w)
        for b in writes:
            need(b.w)
            for k, v in b.r.items():
                need((k, v))
            if b.parent is not None:
                need(b.parent.w)
                for k, v in b.parent.r.items():
                    need((k, v))
            for c in b.children:
                need(c.w)
                for k, v in c.r.items():
                    need((k, v))
        if slot is not None and self.cnt[slot] > 0:
            k0 = (slot, 0)
            if deps.get(k0, 0) < self.cnt[slot]:
                deps[k0] = self.cnt[slot]
        for k, v in deps.items():
            if self.waited[e].get(k, 0) < v:
                self.prog[e].append(("wait", k, v))
                self.waited[e][k] = v
        if slot is None:
            c = self.cnt[e]
            key = (e, c // EPOCH)
            val = c % EPOCH + 1
            self.cnt[e] = c + 1
            inc = 1
        else:
            self.cnt[slot] += 16
            key = (slot, 0)
            val = self.cnt[slot]
            inc = 16
        self._sem(key)
        tok = (key, val)
        self.prog[e].append(("op", fn, key, inc))
        for b in writes:
            b.w = tok
            b.r = {}
        for b in reads:
            if b.r.get(key, 0) < val:
                b.r[key] = val
        self.nops += 1
        return tok

    def final_wait(self, e, bufs):
        for b in bufs:
            if b.w is not None:
                k, v = b.w
                self.prog[e].append(("wait", k, v))

    def emit(self):
        nc = self.nc
        sched = self
        with nc.Block() as block:
            def run(engname, eng):
                for item in sched.prog[engname]:
                    if item[0] == "wait":
                        eng.wait_ge(sched.sem[item[1]], item[2])
                    else:
                        _, fn, k, inc = item
                        fn(eng).then_inc(sched.sem[k], inc)

            @block.tensor
            def _(eng):
                run("pe", eng)

            @block.scalar
            def _(eng):
                run("act", eng)

            @block.vector
            def _(eng):
                run("dve", eng)

            @block.gpsimd
            def _(eng):
                run("pool", eng)

            @block.sync
            def _(eng):
                run("sp", eng)


def build(NPRE, NMAIN):
    NT = NPRE + NMAIN
    NSW = NMAIN + 1
    nc = bass.Bass("TRN2", target_bir_lowering=False)

    def din(name, shape):
        return nc.dram_tensor(name, list(shape), F32, kind="ExternalInput").ap()

    xT_d = din("xT", [NT, 128, 1024])
    xr_d = din("xr", [NMAIN, 128, 1024])
    wf_d = din("wf", [128, 8 * NF])
    wt_d = din("wt", [128, 8 * NTM])
    wb_d = din("wb", [128, 8192])
    wo_d = din("wo", [128, 8192])
    c32_d = din("c32", [128, NC32])
    c16_d = din("c16", [128, NC16])
    convw_d = din("convw", [128, 48])
    small_d = din("small", [128, 12])
    normw_d = din("normw", [1, 128])
    lng_d = din("lng", [1, 1024])
    lnb_d = din("lnb", [1, 1024])
    rope_d = din("rope", [128, NSW * 32])
    out_d = nc.dram_tensor("out", [NMAIN, 128, 1024], F32, kind="ExternalOutput").ap()

    with ExitStack() as es:
        def sb(name, shape, dt):
            return es.enter_context(nc.sbuf_tensor("sb_" + name, list(shape), dt))

        def ps(name, shape, dt):
            return es.enter_context(nc.psum_tensor("pp_" + name, list(shape), dt))

        S = Sched(nc, es)
        B = {}

        def buf(name, excl=False, parent=None):
            B[name] = Buf(name, excl, parent)
            return B[name]

        Wf = sb("Wf", [128, 8, NF], BF16); bWf = buf("Wf")
        Wt = sb("Wt", [128, 8, NTM], BF16); bWt = buf("Wt")
        Wb = sb("Wb", [128, 2, 4, 1024], BF16); bWb = buf("Wb")
        Wo = sb("Wo", [128, 8, 1024], BF16); bWo = buf("Wo")
        c32 = sb("c32", [128, NC32], F32); bc32 = buf("c32")
        c16 = sb("c16", [128, NC16], BF16); bc16 = buf("c16")
        convw = sb("convw", [128, 12, 4], F32); bconvw = buf("convw")
        small = sb("small", [128, 12], F32); bsmall = buf("small")
        cvec = sb("cvec", [128, 12], F32); bcvec = buf("cvec")
        normw = sb("normw", [128, 128], F32); bnormw = buf("normw")
        lng = sb("lng", [128, 1024], F32); blng = buf("lng")
        lnb = sb("lnb", [128, 1024], F32); blnb = buf("lnb")
        ropeT2 = sb("ropeT2", [128, 2, 32], F32)
        PTb = sb("PTb", [128, 2, 8, 128], BF16)
        tokg = sb("tokg", [128, 2, 8], F32)
        S32 = sb("S32", [128, 4, 128], F32); bS32 = buf("S32")
        S16 = sb("S16", [128, 4, 128], BF16); bS16 = buf("S16")
        halo = sb("halo", [128, 12, 3], F32); bhalo = buf("halo")
        kTr = sb("kTr", [128, 2, 128], BF16); bkTr = [buf("kTr0"), buf("kTr1")]
        vtok = sb("vtok", [128, 2, 128], BF16); bvtok = [buf("vtok0"), buf("vtok1")]
        slotA = sb("slotA", [128, 1572], F32); bA = buf("slotA")
        slotB = sb("slotB", [128, 1536], F32); bB = buf("slotB")
        slotC = sb("slotC", [128, 1536], F32); bC = buf("slotC")
        xr = sb("xr", [128, 1024], F32); bxr = buf("xr")
        xfb = sb("xfb", [128, 1024], F32); bxf = buf("xfb")
        xb2 = sb("xb2", [128, 2, 8, 128], BF16); bxb2 = [buf("xb_0"), buf("xb_1")]
        qkv = sb("qkv", [128, 12, 128], BF16); bqkv = buf("qkv")
        zA = sb("zA", [128, 4, 128], BF16); bzA = buf("zA")
        zB = sb("zB", [128, 4, 128], BF16); bzB = buf("zB")
        gt = sb("gt", [128, 16, 128], BF16); bgt = buf("gt")
        swq = sb("swq", [128, 512], BF16); bswq = buf("swq")
        swk = sb("swk", [128, 128], BF16); bswk = buf("swk")
        qT4 = sb("qT4", [128, 4, 128], BF16); bqT4 = buf("qT4")
        rtmp = sb("rtmp", [128, 2, 10, 16], F32); brtmp = buf("rtmp")
        tokd = sb("tokd", [128, 2, 96], F32); btokd = [buf("tok0"), buf("tok1")]
        vecsd = sb("vecsd", [128, 2, 3, 4], F32); bvecsd = [buf("vecs0"), buf("vecs1")]
        cvt = sb("cvt", [128, 4, 128], F32); bcvt = buf("cvt")
        acck = sb("acck", [128, 4, 128], F32)
        intraT = sb("intraT", [128, 4, 128], BF16); bintra = buf("intraT")
        qdT = sb("qdT", [128, 4, 128], BF16); bqd = buf("qdT")
        TT = sb("TT", [128, 4, 128], BF16); bTT = buf("TT")
        kbg = sb("kbg", [128, 4, 128], BF16); bkbg = buf("kbg")
        kt = sb("kt", [128, 4, 128], BF16); bkt = buf("kt")
        vb = sb("vb", [128, 4, 128], BF16); bvb = buf("vb")
        wT = sb("wT", [128, 4, 128], BF16); bwT = buf("wT")
        u32 = sb("u32", [128, 4, 128], F32); bu = buf("u32")
        vnew = sb("vnew", [128, 4, 128], BF16); bvnew = buf("vnew")
        o32 = sb("o32", [128, 4, 128], F32); bo = buf("o32")
        on16 = sb("on16", [128, 4, 128], BF16); bon = buf("on16")
        yaT = sb("yaT", [128, 4, 128], BF16); bya = buf("yaT")
        ybT = sb("ybT", [128, 4, 128], BF16); byb = buf("ybT")
        mg = sb("mg", [128, 8, 128], BF16); bmg = buf("mg")
        r32 = sb("r32", [128, 1024], F32); br = buf("r32")
        br_lo = buf("r32lo", parent=br); br_hi = buf("r32hi", parent=br)
        st = sb("st", [128, 16], F32); bst = buf("st")

        hq = slotA[:, 0:1572].rearrange("p (c t) -> p c t", c=12)
        D3 = slotA[:, 0:1536].rearrange("p (k h t) -> p k h t", k=3, h=4)
        PT = slotA[:, 0:1024].bitcast(BF16).rearrange("p (b h q) -> p b h q", b=2, h=8)
        acc = slotB[:, 0:1536].rearrange("p (c t) -> p c t", c=12)
        Ees = slotB[:, 0:1024].bitcast(BF16).rearrange("p (k h t) -> p k h t", k=4, h=4)
        sq16 = slotB[:, 1024:1536].bitcast(BF16).rearrange("p (c t) -> p c t", c=8)
        xf = xfb[:, :].rearrange("p (c t) -> p c t", c=8)
        chX = slotC[:, 0:512].bitcast(BF16).rearrange("p (s h t) -> p s h t", s=2, h=4)
        chYQ = slotC[:, 512:1536].bitcast(BF16).rearrange("p (s h w t) -> p s h w t", s=2, h=4, w=2)
        tmpa = slotC[:, 0:512].rearrange("p (h t) -> p h t", h=4)
        tmpb = slotC[:, 512:1024].rearrange("p (h t) -> p h t", h=4)
        tmpc = slotC[:, 1024:1536].rearrange("p (h t) -> p h t", h=4)
        cvtp = r32[:, 512:1024].rearrange("p (h t) -> p h t", h=4)

        PS = [ps("ps%d" % i, [128, 512], F32) for i in range(8)]
        bPS = [buf("ps%d" % i, True) for i in range(8)]

        def P4(i):
            return PS[i][:, :].rearrange("p (h t) -> p h t", h=4)

        ld = S.dma_slot("ld")
        ldx = S.dma_slot("ldx")
        ldr = S.dma_slot("ldr")
        stq = S.dma_slot("st")

        id32 = c32[:, C_ID:C_ID + 128]
        ones32 = c32[:, C_ONE:C_ONE + 128]
        id16 = c16[:, K_ID:K_ID + 128]
        ones16 = c16[:, K_ONE:K_ONE + 64]

        S.op("sp", lambda e: e.dma_start(out=c32[:, :], in_=c32_d), writes=[bc32], slot=S.dma_slot("ldc1"))
        S.op("sp", lambda e: e.dma_start(out=r32[:, 0:NC16], in_=c16_d), writes=[br], slot=S.dma_slot("ldc2"))
        S.op("dve", lambda e: e.tensor_copy(out=c16[:, :], in_=r32[:, 0:NC16]), reads=[br], writes=[bc16])
        S.op("sp", lambda e: e.dma_start(out=convw[:, :, :].rearrange("p c j -> p (c j)"), in_=convw_d), writes=[bconvw], slot=S.dma_slot("ldc3"))
        S.op("sp", lambda e: e.dma_start(out=small[:, :], in_=small_d), writes=[bsmall], slot=S.dma_slot("ldc4"))
        S.op("sp", lambda e: e.dma_start(out=normw[:, :], in_=normw_d.partition_broadcast(128)), writes=[bnormw], slot=S.dma_slot("ldc5"))
        S.op("sp", lambda e: e.dma_start(out=lng[:, :], in_=lng_d.partition_broadcast(128)), writes=[blng], slot=S.dma_slot("ldc6"))
        S.op("sp", lambda e: e.dma_start(out=lnb[:, :], in_=lnb_d.partition_broadcast(128)), writes=[blnb], slot=S.dma_slot("ldc7"))
        S.op("act", lambda e: e.activation(out=cvec[:, 0:4], in_=small[:, 0:4], func=AF.Exp), reads=[bsmall], writes=[bcvec])
        S.op("dve", lambda e: e.tensor_scalar_mul(out=cvec[:, 0:4], in0=cvec[:, 0:4], scalar1=-1.0), reads=[bcvec], writes=[bcvec])
        S.op("dve", lambda e: e.tensor_copy(out=cvec[:, 4:8], in_=small[:, 4:8]), reads=[bsmall], writes=[bcvec])
        S.op("act", lambda e: e.activation(out=cvec[:, 8:12], in_=small[:, 8:12], func=AF.Exp), reads=[bsmall], writes=[bcvec])
        S.op("pool", lambda e: e.memset(halo[:, :, :], 0.0), writes=[bhalo])
        S.op("pool", lambda e: e.memset(kTr[:, :, :], 0.0), writes=bkTr)
        S.op("pool", lambda e: e.memset(vtok[:, :, :], 0.0), writes=bvtok)

        stg = [(xr[:, :], bxr), (r32[:, :], br), (slotC[:, 0:1024], bC)]
        stg_slots = [S.dma_slot("stg%d" % i) for i in range(3)]
        casters = ["dve", "pool", "act"]
        pieces = []
        Wf_flat = Wf[:, :, :].rearrange("p k c -> p (k c)")
        Wt_flat = Wt[:, :, :].rearrange("p k c -> p (k c)")
        Wb_flat = Wb[:, :, :, :].rearrange("p a b c -> p (a b c)")
        Wo_flat = Wo[:, :, :].rearrange("p k c -> p (k c)")
        for (src, dst, bdst, total) in ((wf_d, Wf_flat, bWf, 8 * NF), (wt_d, Wt_flat, bWt, 8 * NTM),
                                        (wb_d, Wb_flat, bWb, 8192), (wo_d, Wo_flat, bWo, 8192)):
            o = 0
            while o < total:
                w = min(1024, total - o)
                pieces.append((src, dst, bdst, o, w))
                o += w
        for i, (src, dst, bdst, o, w) in enumerate(pieces):
            sap, sbuf_ = stg[i % 3]
            ce = casters[i % 3]
            S.op("sp", lambda e, sap=sap, src=src, o=o, w=w: e.dma_start(out=sap[:, 0:w], in_=src[:, o:o + w]),
                 writes=[sbuf_], slot=stg_slots[i % 3])
            if ce == "act":
                S.op("act", lambda e, sap=sap, dst=dst, o=o, w=w: e.copy(out=dst[:, o:o + w], in_=sap[:, 0:w]),
                     reads=[sbuf_], writes=[bdst])
            else:
                S.op(ce, lambda e, sap=sap, dst=dst, o=o, w=w: e.tensor_copy(out=dst[:, o:o + w], in_=sap[:, 0:w]),
                     reads=[sbuf_], writes=[bdst])

        def mm(out, lhsT, rhs, start, stop, reads, writes, tp=None):
            if tp is None:
                S.op("pe", lambda e: e.matmul(out, lhsT=lhsT, rhs=rhs, start=start, stop=stop), reads=reads, writes=writes)
            else:
                S.op("pe", lambda e: e.matmul(out, lhsT=lhsT, rhs=rhs, start=start, stop=stop, tile_position=tp),
                     reads=reads, writes=writes)

        def tr(out, in_, reads, writes):
            S.op("pe", lambda e: e.transpose(out, in_, id16), reads=list(reads) + [bc16], writes=writes)

        def act(out, in_, func, reads, writes, bias=None, scale=None, accum=None):
            kw = {}
            if bias is not None:
                kw["bias"] = bias
            if scale is not None:
                kw["scale"] = scale
            if accum is not None:
                kw["accum_out"] = accum
            S.op("act", lambda e: e.activation(out=out, in_=in_, func=func, **kw), reads=reads, writes=writes)

        def tt(eng, out, in0, in1, op, reads, writes):
            S.op(eng, lambda e: e.tensor_tensor(out=out, in0=in0, in1=in1, op=op), reads=reads, writes=writes)

        def stt(out, in0, scalar, in1, op0, op1, reads, writes):
            S.op("dve", lambda e: e.scalar_tensor_tensor(out=out, in0=in0, scalar=scalar, in1=in1, op0=op0, op1=op1),
                 reads=reads, writes=writes)

        def bc_last(ap2, n):
            return ap2.unsqueeze(2).to_broadcast([128, ap2.shape[1], n])

        def bc_mid(ap2, k):
            return ap2.unsqueeze(1).to_broadcast([128, k, ap2.shape[1]])

        T_BETA, T_XA, T_EX, T_SP, T_G, T_LNB, T_LNK, T_LNQ, T_EA, T_EKT, T_EGL, T_TMP, T_MS, T_RSTD, T_GC = (
            0, 4, 8, 12, 16, 20, 24, 28, 32, 36, 40, 48, 52, 56, 60)


        def PBh(i):
            return PS[i][:, :].bitcast(BF16).rearrange("p (c t) -> p c t", c=8)

        GN = ("Ees", "chX", "chY", "chQ", "intra", "qd", "TT", "kbg", "kt", "vb", "wT", "u", "vnew", "o", "on", "ya",
              "S32", "S16", "tokg")
        baccK = buf("accK")
        baccV = buf("accV")
        bacc = [bmg, baccK, baccV]
        accv = [mg[:, :, :].rearrange("p c t -> p (c t)").bitcast(F32).rearrange("p (c t) -> p c t", c=4),
                acck[:, :, :], o32[:, :, :]]
        gpar = {"Ees": bB, "chX": bC, "chY": bC, "chQ": bC, "o": baccV}
        gB = [{nm: buf("%s_g%d" % (nm, g), parent=gpar.get(nm)) for nm in GN} for g in range(2)]
        bsq = buf("sq16c", parent=bB)
        bPTb = buf("PTbuf")
        bropeT2 = [buf("ropeT_0"), buf("ropeT_1")]
        ldrope = S.dma_slot("ldrope")
        S.op("pool", lambda e: e.memset(S32[:, :, :], 0.0), writes=[gB[0]["S32"], gB[1]["S32"]])
        S.op("pool", lambda e: e.memset(S16[:, :, :], 0.0), writes=[gB[0]["S16"], gB[1]["S16"]])
        flags = {}

        def tile_info(n):
            is_main = n >= NPRE
            last_pre = (n == NPRE - 1)
            return dict(is_main=is_main, m=n - NPRE, last_pre=last_pre, need_sw=is_main or last_pre,
                        need_qh=is_main or last_pre, ti=n - (NPRE - 1), cur=n % 2, prv=1 - n % 2,
                        ch0=0 if is_main else 4)

        def rope_ops(src, dst, nh, rt, bsrc, bdst, tcol, brt):
            cc = rt[:, 0:16]
            ss_a = rt[:, 16:24]
            ss_b = rt[:, 24:32]
            t1 = rtmp[:, 0, tcol:tcol + nh, :]
            t2 = rtmp[:, 1, tcol:tcol + nh, :]
            S.op("dve", lambda e: e.tensor_copy(out=dst[:, :, 16:64], in_=src[:, :, 16:64]), reads=[bsrc], writes=[bdst])
            tt("dve", t1, src[:, :, 0:16], bc_mid(cc, nh), ALU.mult, [bsrc, brt], [brtmp])
            tt("dve", t2[:, :, 0:8], src[:, :, 8:16], bc_mid(ss_a, nh), ALU.mult, [bsrc, brt], [brtmp])
            tt("dve", t2[:, :, 8:16], src[:, :, 0:8], bc_mid(ss_b, nh), ALU.mult, [bsrc, brt], [brtmp])
            tt("dve", dst[:, :, 0:16], t1, t2, ALU.add, [brtmp], [bdst])

        def wait_flags(keys):
            while not all(flags.get(k) for k in keys):
                yield

        def early(n, streams=None):
            I = tile_info(n)
            is_main, need_sw, need_qh, ch0, cur, last_pre = I["is_main"], I["need_sw"], I["need_qh"], I["ch0"], I["cur"], I["last_pre"]
            tok = tokd[:, n % 2, :]
            vecs = vecsd[:, n % 2]
            xb = xb2[:, n % 2]
            bxb = bxb2[n % 2]

            def tk(c, w=4):
                return tok[:, c:c + w]
            pn = n - 1
            if need_sw:
                ti = I["ti"]
                S.op("sp", lambda e: e.dma_start(out=ropeT2[:, n % 2, :], in_=rope_d[:, ti * 32:(ti + 1) * 32]),
                     writes=[bropeT2[n % 2]], slot=ldrope)
            for kc in range(8):
                mm(PS[7][:, 256:264], xb[:, kc, :], Wt[:, kc, 768:776], kc == 0, kc == 7, [bxb, bWt], [bPS[7]])
            act(tk(T_TMP), PS[7][:, 256:260], AF.Exp, [bPS[7]], [btokd[n % 2]], scale=-1.0)
            tt("dve", tk(T_XA), PS[7][:, 260:264], cvec[:, 4:8], ALU.add, [bPS[7], bcvec], [btokd[n % 2]])
            act(tk(T_LNB), tk(T_TMP), AF.Ln, [btokd[n % 2]], [btokd[n % 2]], bias=1.0)
            act(tk(T_BETA), tk(T_LNB), AF.Exp, [btokd[n % 2]], [btokd[n % 2]], scale=-1.0)
            yield
            if pn >= 0:
                yield from wait_flags([("d3", pn, 0), ("d3", pn, 1)])
            S.op("pool", lambda e: e.tensor_copy(out=hq[:, ch0:12, 0:3], in_=halo[:, ch0:12, :]), reads=[bhalo], writes=[bA])
            qg = ([0] if need_qh else []) + [1, 2]
            def conv_steps(cg):
                eng = "pool" if cg == 2 else "dve"
                ct = cvtp if cg == 2 else cvt[:, :, :]
                bct = br_hi if cg == 2 else bcvt
                cs = slice(cg * 4, cg * 4 + 4)
                for j in range(4):
                    wj = convw[:, cs, j:j + 1].to_broadcast([128, 4, 128])
                    if j == 0:
                        tt(eng, accv[cg], hq[:, cs, 0:128], wj, ALU.mult, [bA, bconvw], [bacc[cg]])
                    else:
                        tt(eng, ct, hq[:, cs, j:j + 128], wj, ALU.mult, [bA, bconvw], [bct])
                        tt(eng, accv[cg], accv[cg], ct, ALU.add, [bacc[cg], bct], [bacc[cg]])
                    yield

            ci = 0
            pend = []
            for g in qg:
                for j in range(4):
                    pb = 6 + ci % 2
                    col = (g * 4 + j) * 128
                    for kc in range(8):
                        mm(PS[pb][:, 0:128], Wf[:, kc, col:col + 128], xb[:, kc, :], kc == 0, kc == 7, [bWf, bxb], [bPS[pb]])
                    if ci % 2 == 0:
                        S.op("act", lambda e, c=g * 4 + j, pb=pb: e.copy(out=hq[:, c, 3:131], in_=PS[pb][:, 0:128]),
                             reads=[bPS[pb]], writes=[bA])
                    else:
                        S.op("dve", lambda e, c=g * 4 + j, pb=pb: e.tensor_copy(out=hq[:, c, 3:131], in_=PS[pb][:, 0:128]),
                             reads=[bPS[pb]], writes=[bA])
                    ci += 1
                    for cgen in list(pend):
                        try:
                            next(cgen)
                        except StopIteration:
                            pend.remove(cgen)
                    yield
                if is_main or g > 0:
                    pend.append(conv_steps(g))
            hc0 = 0 if need_qh else 4
            S.op("pool", lambda e: e.tensor_copy(out=halo[:, hc0:12, :], in_=hq[:, hc0:12, 128:131]), reads=[bA], writes=[bhalo])
            flags[("fm", n)] = True
            if not is_main and last_pre:
                for kc in range(8):
                    mm(PS[7][:, 0:256], xb[:, kc, :], Wt[:, kc, 512:768], kc == 0, kc == 7, [bxb, bWt], [bPS[7]])
                srck = PS[7][:, 0:128].rearrange("p (g d) -> p g d", g=2)
                dstk = swk[:, :].rearrange("p (g d) -> p g d", g=2)
                rope_ops(srck, dstk, 2, ropeT2[:, n % 2, :], bPS[7], bswk, 8, bropeT2[n % 2])
                S.op("act", lambda e: e.copy(out=vtok[:, cur, :], in_=PS[7][:, 128:256]), reads=[bPS[7]], writes=[bvtok[cur]])
                tr(PBh(6)[:, 4, :], swk[:, :], [bswk], [bPS[6]])
                S.op("act", lambda e: e.copy(out=kTr[:, cur, :], in_=PBh(6)[:, 4, :]), reads=[bPS[6]], writes=[bkTr[cur]])
            yield
            while pend:
                for cgen in list(pend):
                    try:
                        next(cgen)
                    except StopIteration:
                        pend.remove(cgen)
                yield
            cgs = ([0] if is_main else []) + [1, 2]
            if pn >= 0:
                yield from wait_flags([("ph5", pn, 0), ("ph5", pn, 1), ("ees", pn, 0), ("ees", pn, 1)])
            for cg in cgs:
                act(qkv[:, cg * 4:cg * 4 + 4, :], accv[cg], AF.Silu, [bacc[cg]], [bqkv])
            flags[("silu", n)] = True
            if pn >= 0 and tile_info(pn)["is_main"]:
                yield from wait_flags([("lf", pn)])
                zgate_act(pn)
            yield
            sq0 = 0 if is_main else 4
            tt("dve", sq16[:, sq0:8, :], qkv[:, sq0:8, :], qkv[:, sq0:8, :], ALU.mult, [bqkv], [bsq])
            for c in range(sq0, 8):
                mm(PS[7][:, c:c + 1], sq16[:, c, :], ones16[:, 0:1], True, True, [bsq, bc16], [bPS[7]])
            bt = btokd[n % 2]
            bv = bvecsd[n % 2]
            act(tk(T_EX), tk(T_XA), AF.Exp, [bt], [bt])
            act(tk(T_LNK), PS[7][:, 4:8], AF.Ln, [bPS[7]], [bt], bias=NORM_EPS)
            if is_main:
                act(tk(T_LNQ), PS[7][:, 0:4], AF.Ln, [bPS[7]], [bt], bias=128.0 * NORM_EPS, scale=128.0)
            yield
            act(tk(T_SP), tk(T_EX), AF.Ln, [bt], [bt], bias=1.0)
            yield
            tt("dve", tk(T_G), tk(T_SP), cvec[:, 0:4], ALU.mult, [bt, bcvec], [bt])
            yield
            for i_, cm in enumerate((C_TRI, C_SOWN, C_SA0, C_SA1)):
                mm(PS[7][:, 384 + 4 * i_:388 + 4 * i_], c32[:, cm:cm + 128], tk(T_G), True, True, [bc32, bt], [bPS[7]])
            S.op("dve", lambda e: e.tensor_copy(out=tk(T_GC, 16), in_=PS[7][:, 384:400]), reads=[bPS[7]], writes=[bt])
            yield
            stt(tk(T_TMP), tk(T_LNK), -0.5, tk(T_LNB), ALU.mult, ALU.subtract, [bt], [bt])
            yield
            tt("dve", vecs[:, 1, :], tk(T_TMP), tk(T_GC), ALU.add, [bt], [bv])
            yield
            stt(vecs[:, 0, :], tk(T_LNK), -0.5, tk(T_GC), ALU.mult, ALU.subtract, [bt], [bv])
            if is_main:
                yield
                stt(vecs[:, 2, :], tk(T_LNQ), -0.5, tk(T_GC), ALU.mult, ALU.add, [bt], [bv])
            yield
            tt("dve", tk(T_TMP), tk(T_GC + 4), vecs[:, 0, :], ALU.add, [bt, bv], [bt])
            act(tk(T_EA), vecs[:, 1, :], AF.Exp, [bv], [bt])
            yield
            act(tk(T_EKT), tk(T_TMP), AF.Exp, [bt], [bt])
            act(tk(T_EGL, 8), tk(T_GC + 8, 8), AF.Exp, [bt], [bt])
            yield
            tt("dve", D3[:, 0], id32.unsqueeze(1).to_broadcast([128, 4, 128]),
               vecs[:, 0, :].unsqueeze(2).to_broadcast([128, 4, 128]), ALU.mult, [bc32, bv], [bA])
            if is_main:
                tt("dve", D3[:, 2], id32.unsqueeze(1).to_broadcast([128, 4, 128]),
                   vecs[:, 2, :].unsqueeze(2).to_broadcast([128, 4, 128]), ALU.mult, [bc32, bv], [bA])
            yield
            if pn >= 0:
                yield from wait_flags([("ees", pn, 0), ("ees", pn, 1)])
            especs = [(0, K_MS, 1, 0)]
            if is_main:
                especs += [(2, K_MU, 0, 2), (2, None, None, 3)]
            for si, (kind, mk, bias_kind, eidx) in enumerate(especs):
                bank = 6 + si % 2
                for h in range(4):
                    reg = PS[bank][:, h * 128:(h + 1) * 128]
                    mm(reg, ones32, D3[:, kind, h, :], True, mk is None, [bc32, bA], [bPS[bank]])
                    if mk is not None:
                        mm(reg, id16, c16[:, mk:mk + 128], False, True, [bc16], [bPS[bank]])
                if mk is not None:
                    for h in range(4):
                        act(Ees[:, eidx, h, :], PS[bank][:, h * 128:(h + 1) * 128], AF.Exp, [bPS[bank], bv],
                            [gB[h // 2]["Ees"]], bias=vecs[:, bias_kind, h:h + 1])
                else:
                    act(Ees[:, eidx], P4(bank), AF.Exp, [bPS[bank]], [gB[0]["Ees"], gB[1]["Ees"]])
                yield
            if streams is not None:
                streams.append(group(n, 0))
                streams.append(group(n, 1))

        def zgate_act(n):
            act(zA[:, :, :], zA[:, :, :], AF.Silu, [bzA], [bzA])
            act(zB[:, :, :], zB[:, :, :], AF.Silu, [bzB], [bzB])
            act(gt[:, :, :], gt[:, :, :], AF.Tanh, [bgt], [bgt], scale=0.5)
            flags[("zact", n)] = True

        def cast_tile(n):
            S.op("act", lambda e: e.copy(out=xb2[:, n % 2], in_=xf), reads=[bxf], writes=[bxb2[n % 2]])
            if n + 1 < NT:
                S.op("sp", lambda e: e.dma_start(out=xfb[:, :], in_=xT_d[n + 1]), writes=[bxf], slot=ldx)

        def late_front(n, streams=None):
            xb = xb2[:, n % 2]
            bxb = bxb2[n % 2]
            cur = n % 2
            S.op("sp", lambda e: e.dma_start(out=xr[:, :], in_=xr_d[n - NPRE]), writes=[bxr], slot=ldr)
            for kc in range(8):
                mm(PS[6][:, 0:512], xb[:, kc, :], Wt[:, kc, 0:512], kc == 0, kc == 7, [bxb, bWt], [bPS[6]])
            for kc in range(8):
                mm(PS[7][:, 0:256], xb[:, kc, :], Wt[:, kc, 512:768], kc == 0, kc == 7, [bxb, bWt], [bPS[7]])
            srcq = PS[6][:, 0:512].rearrange("p (s c d) -> p s c d", s=2, c=4)
            dstq = swq[:, :].rearrange("p (c s d) -> p s c d", c=4, s=2)
            for s_ in range(2):
                rope_ops(srcq[:, s_], dstq[:, s_], 4, ropeT2[:, n % 2, :], bPS[6], bswq, s_ * 4, bropeT2[n % 2])
            srck = PS[7][:, 0:128].rearrange("p (g d) -> p g d", g=2)
            dstk = swk[:, :].rearrange("p (g d) -> p g d", g=2)
            rope_ops(srck, dstk, 2, ropeT2[:, n % 2, :], bPS[7], bswk, 8, bropeT2[n % 2])
            S.op("act", lambda e: e.copy(out=vtok[:, cur, :], in_=PS[7][:, 128:256]), reads=[bPS[7]], writes=[bvtok[cur]])
            yield
            ci = 0
            for g in (3, 4, 5, 6, 7, 8):
                for j in range(4):
                    pb = 6 + ci % 2
                    col = (g * 4 + j) * 128
                    for kc in range(8):
                        mm(PS[pb][:, 0:128], Wf[:, kc, col:col + 128], xb[:, kc, :], kc == 0, kc == 7, [bWf, bxb], [bPS[pb]])
                    if g == 3:
                        dst, bd = zA[:, j, :], bzA
                    elif g == 4:
                        dst, bd = zB[:, j, :], bzB
                    else:
                        dst, bd = gt[:, (g - 5) * 4 + j, :], bgt
                    if ci % 2 == 0:
                        S.op("act", lambda e, dst=dst, pb=pb: e.copy(out=dst, in_=PS[pb][:, 0:128]), reads=[bPS[pb]], writes=[bd])
                    else:
                        S.op("dve", lambda e, dst=dst, pb=pb: e.tensor_copy(out=dst, in_=PS[pb][:, 0:128]), reads=[bPS[pb]], writes=[bd])
                    ci += 1
                    yield
            flags[("lf", n)] = True
            if n == NT - 1:
                zgate_act(n)
            if streams is not None:
                streams.append(wswa(n))

        def group(n, g):
            I = tile_info(n)
            is_main = I["is_main"]
            G = gB[g]

            def ev(kind, out, in_, reads, writes):
                if g == 0:
                    S.op("act", lambda e: e.copy(out=out, in_=in_), reads=reads, writes=writes)
                else:
                    S.op("dve", lambda e: e.tensor_copy(out=out, in_=in_), reads=reads, writes=writes)
            tok = tokd[:, n % 2, :]
            vecs = vecsd[:, n % 2]
            b0, b1, b2 = 3 * g, 3 * g + 1, 3 * g + 2
            hs = slice(2 * g, 2 * g + 2)
            EG = Ees[:, :, hs, :]
            pbh = PBh(b1)
            for hh in range(2):
                h = 2 * g + hh
                tr(pbh[:, hh, :], qkv[:, 4 + h, :], [bqkv], [bPS[b1]])
                tr(pbh[:, 2 + hh, :], qkv[:, 8 + h, :], [bqkv], [bPS[b1]])
            flags[("ph5", n, g)] = True
            yield
            tt("dve", kbg[:, hs, :], pbh[:, 0:2, :], bc_last(tok[:, T_EA + 2 * g:T_EA + 2 * g + 2], 128), ALU.mult,
               [bPS[b1], btokd[n % 2]], [G["kbg"]])
            tt("dve", vb[:, hs, :], pbh[:, 2:4, :], bc_last(tok[:, T_BETA + 2 * g:T_BETA + 2 * g + 2], 128), ALU.mult,
               [bPS[b1], btokd[n % 2]], [G["vb"]])
            tt("dve", kt[:, hs, :], pbh[:, 0:2, :], bc_last(tok[:, T_EKT + 2 * g:T_EKT + 2 * g + 2], 128), ALU.mult,
               [bPS[b1], btokd[n % 2]], [G["kt"]])
            yield
            for hh in range(2):
                h = 2 * g + hh
                mm(PS[b2][:, hh * 128:(hh + 1) * 128], qkv[:, 4 + h, :], qkv[:, 4 + h, :], True, True, [bqkv], [bPS[b2]])
            if is_main:
                for hh in range(2):
                    h = 2 * g + hh
                    mm(PS[b2][:, 256 + hh * 128:256 + (hh + 1) * 128], qkv[:, 4 + h, :], qkv[:, h, :], True, True,
                       [bqkv], [bPS[b2]])
            flags[("d3", n, g)] = True
            yield
            Graw = PS[b2][:, 0:256].rearrange("p (h t) -> p h t", h=2)
            stt(chX[:, 0, hs], Graw, -1.0, EG[:, 0], ALU.mult, ALU.mult, [bPS[b2], G["Ees"]], [G["chX"]])
            pbt = PBh(b0)
            for hh in range(2):
                tr(pbt[:, 4 + hh, :], chX[:, 0, 2 * g + hh, :], [G["chX"]], [bPS[b0]])
            ev("copy", chYQ[:, 0, hs, 0, :], pbt[:, 4:6, :], [bPS[b0]], [G["chY"]])
            if is_main:
                tt("dve", intraT[:, hs, :], PS[b2][:, 256:512].rearrange("p (h t) -> p h t", h=2), EG[:, 2], ALU.mult,
                   [bPS[b2], G["Ees"]], [G["intra"]])
            yield
            tt("dve", chYQ[:, 0, hs, 1, :], chYQ[:, 0, hs, 0, :], bc_mid(id16, 2), ALU.add, [G["chY"], bc16], [G["chQ"]])
            if is_main:
                tt("dve", qdT[:, hs, :], qkv[:, hs, :], EG[:, 3], ALU.mult, [bqkv, G["Ees"]], [G["qd"]])
            flags[("ees", n, g)] = True
            yield
            chb = [G["chX"], G["chY"], G["chQ"]]
            for k in range(6):
                s0 = k % 2
                s1 = 1 - s0
                bk = b0 if k % 2 == 0 else b1
                if k < 5:
                    for hh in range(2):
                        h = 2 * g + hh
                        off = hh * 256
                        mm(PS[bk][:, off:off + 128], chX[:, s0, h, :], chYQ[:, s0, h, 0, :], True, True, chb, [bPS[bk]])
                        if k > 0:
                            mm(PS[bk][:, off + 128:off + 256], chX[:, s0, h, :], chYQ[:, s0, h, 1, :], True, False, chb, [bPS[bk]])
                            mm(PS[bk][:, off + 128:off + 256], id16, chYQ[:, s0, h, 1, :], False, True, chb + [bc16], [bPS[bk]])
                        mm(PS[b2][:, hh * 128:(hh + 1) * 128], chYQ[:, s0, h, 0, :], chX[:, s0, h, :], True, True,
                           chb, [bPS[b2]])
                    yield
                    pv = PS[bk][:, :].rearrange("p (h w t) -> p h w t", h=2, w=2)
                    ev("copy", chX[:, s1, hs], PS[b2][:, 0:256].rearrange("p (h t) -> p h t", h=2), [bPS[b2]], [G["chX"]])
                    if k == 0:
                        S.op("dve", lambda e, s0=s0, s1=s1: e.tensor_copy(out=chYQ[:, s1, hs, 1, :], in_=chYQ[:, s0, hs, 1, :]),
                             reads=[G["chQ"]], writes=[G["chQ"]])
                        ev("copy", chYQ[:, s1, hs, 0, :], pv[:, :, 0, :], [bPS[bk]], [G["chY"]])
                    else:
                        ev("copy", chYQ[:, s1, hs, :, :], pv, [bPS[bk]], [G["chY"], G["chQ"]])
                    yield
                else:
                    for hh in range(2):
                        h = 2 * g + hh
                        mm(PS[bk][:, hh * 128:(hh + 1) * 128], chX[:, s0, h, :], chYQ[:, s0, h, 1, :], True, False, chb, [bPS[bk]])
                        mm(PS[bk][:, hh * 128:(hh + 1) * 128], id16, chYQ[:, s0, h, 1, :], False, True, chb + [bc16], [bPS[bk]])
                    yield
                    ev("copy", TT[:, hs, :], PS[bk][:, 0:256].rearrange("p (h t) -> p h t", h=2), [bPS[bk]], [G["TT"]])
                    yield
            for hh in range(2):
                h = 2 * g + hh
                mm(PS[b1][:, hh * 128:(hh + 1) * 128], kbg[:, h, :], TT[:, h, :], True, True, [G["kbg"], G["TT"]], [bPS[b1]])
                mm(PS[b1][:, 256 + hh * 128:256 + (hh + 1) * 128], TT[:, h, :], vb[:, h, :], True, True, [G["TT"], G["vb"]],
                   [bPS[b1]])
            yield
            S.op("act", lambda e: e.copy(out=wT[:, hs, :], in_=PS[b1][:, 0:256].rearrange("p (h t) -> p h t", h=2)),
                 reads=[bPS[b1]], writes=[G["wT"]])
            S.op("act", lambda e: e.copy(out=u32[:, hs, :], in_=PS[b1][:, 256:512].rearrange("p (h t) -> p h t", h=2)),
                 reads=[bPS[b1]], writes=[G["u"]])
            yield
            for c in range(2):
                r0 = c * 64
                rs = slice(r0, r0 + 64)
                for hh in range(2):
                    h = 2 * g + hh
                    mm(PS[b2][rs, hh * 128:(hh + 1) * 128], wT[:, h, rs], S16[:, h, :], True, True, [G["wT"], G["S16"]],
                       [bPS[b2]], tp=(0, r0))
                yield
                tt("dve", vnew[rs, hs, :], u32[rs, hs, :], PS[b2][rs, 0:256].rearrange("p (h t) -> p h t", h=2), ALU.subtract,
                   [G["u"], bPS[b2]], [G["vnew"]])
                tt("dve", S32[:, hs, :], S32[:, hs, :], bc_last(tok[:, T_EGL + 4 * c + 2 * g:T_EGL + 4 * c + 2 * g + 2], 128),
                   ALU.mult, [G["S32"], btokd[n % 2]], [G["S32"]])
                yield
                if is_main and c == 0 and n + 1 < NT:
                    yield from wait_flags([("silu", n + 1)])
                if is_main:
                    for hh in range(2):
                        h = 2 * g + hh
                        mm(PS[b0][rs, hh * 128:(hh + 1) * 128], qdT[:, h, rs], S16[:, h, :], True, False, [G["qd"], G["S16"]],
                           [bPS[b0]], tp=(0, r0))
                        mm(PS[b0][rs, hh * 128:(hh + 1) * 128], intraT[rs, h, rs], vnew[rs, h, :], False, True,
                           [G["intra"], G["vnew"]], [bPS[b0]], tp=(r0, r0))
                for hh in range(2):
                    h = 2 * g + hh
                    mm(PS[b1][:, hh * 128:(hh + 1) * 128], kt[rs, h, :], vnew[rs, h, :], True, True, [G["kt"], G["vnew"]],
                       [bPS[b1]], tp=(r0, 0))
                yield
                tt("dve", S32[:, hs, :], S32[:, hs, :], PS[b1][:, 0:256].rearrange("p (h t) -> p h t", h=2), ALU.add,
                   [G["S32"], bPS[b1]], [G["S32"]])
                S.op("act", lambda e: e.copy(out=S16[:, hs, :], in_=S32[:, hs, :]), reads=[G["S32"]], writes=[G["S16"]])
                if is_main:
                    S.op("act", lambda e, rs=rs: e.copy(out=o32[rs, hs, :], in_=PS[b0][rs, 0:256].rearrange("p (h t) -> p h t", h=2)),
                         reads=[bPS[b0]], writes=[G["o"]])
                yield
            if not is_main:
                return
            tg = tokg[:, g, :]
            for hh in range(2):
                h = 2 * g + hh
                act(on16[:, h, :], o32[:, h, :], AF.Square, [G["o"]], [G["on"], G["tokg"]], accum=tg[:, hh:hh + 1])
            act(tg[:, 2:4], tg[:, 0:2], AF.Ln, [G["tokg"]], [G["tokg"]], bias=NORM_EPS, scale=1.0 / 128.0)
            act(tg[:, 2:4], tg[:, 2:4], AF.Exp, [G["tokg"]], [G["tokg"]], scale=-0.5)
            yield
            tt("dve", o32[:, hs, :], o32[:, hs, :], bc_last(tg[:, 2:4], 128), ALU.mult, [G["o"], G["tokg"]], [G["o"]])
            tt("dve", on16[:, hs, :], o32[:, hs, :], bc_mid(normw[:, :], 2), ALU.mult, [G["o"], bnormw], [G["on"]])
            yield
            yield
            yield from wait_flags([("zact", n)])
            pbh2 = PBh(b2)
            for hh in range(2):
                h = 2 * g + hh
                tr(pbh2[:, hh, :], on16[:, h, :], [G["on"]], [bPS[b2]])
            tt("dve", yaT[:, hs, :], pbh2[:, 0:2, :], zA[:, hs, :], ALU.mult, [bPS[b2], bzA], [G["ya"]])
            yield

        def wswa(n):
            I = tile_info(n)
            m, cur, prv = I["m"], I["cur"], I["prv"]
            p6 = PBh(6)
            for c in range(4):
                tr(p6[:, c, :], swq[:, c * 128:(c + 1) * 128], [bswq], [bPS[6]])
            tr(p6[:, 4, :], swk[:, :], [bswk], [bPS[6]])
            S.op("act", lambda e: e.copy(out=qT4[:, :, :], in_=p6[:, 0:4, :]), reads=[bPS[6]], writes=[bqT4])
            S.op("act", lambda e: e.copy(out=kTr[:, cur, :], in_=p6[:, 4, :]), reads=[bPS[6]], writes=[bkTr[cur]])
            yield
            for blk in range(2):
                kslot = prv if blk == 0 else cur
                if blk == 1:
                    mk = K_MC
                else:
                    mk = K_MP0 if m == 0 else K_MP
                for h in range(8):
                    s_, c_ = h // 4, h % 4
                    bank = 6 + s_
                    reg = PS[bank][:, c_ * 128:(c_ + 1) * 128]
                    mm(reg, kTr[s_ * 64:(s_ + 1) * 64, kslot, :], qT4[s_ * 64:(s_ + 1) * 64, c_, :], True, False,
                       [bkTr[kslot], bqT4], [bPS[bank]], tp=(s_ * 64, 0))
                    mm(reg, id16, c16[:, mk:mk + 128], False, True, [bc16], [bPS[bank]])
                for s_ in range(2):
                    act(PTb[:, blk, s_ * 4:(s_ + 1) * 4, :], P4(6 + s_), AF.Exp, [bPS[6 + s_]], [bPTb], scale=0.125)
                yield
            for h in range(8):
                g_ = h // 4
                po = (h % 2) * 64
                co = (h // 2) * 128
                for blk in range(2):
                    vslot = prv if blk == 0 else cur
                    mm(PS[6][po:po + 64, co:co + 128], vtok[:, vslot, g_ * 64:(g_ + 1) * 64], PTb[:, blk, h, :], blk == 0, blk == 1,
                       [bvtok[vslot], bPTb], [bPS[6]], tp=(0, po))
                for blk in range(2):
                    mm(PS[7][po:po + 64, co:co + 128], ones16, PTb[:, blk, h, :], blk == 0, blk == 1, [bc16, bPTb], [bPS[7]], tp=(0, po))
            wt_ = r32[:, 0:512].rearrange("p (h t) -> p h t", h=4)
            tt("dve", wt_, P4(7), bc_last(cvec[:, 8:12], 128), ALU.add, [bPS[7], bcvec], [br_lo])
            act(wt_, wt_, AF.Ln, [br_lo], [br_lo])
            act(wt_, wt_, AF.Exp, [br_lo], [br_lo], scale=-1.0)
            tt("dve", wt_, P4(6), wt_, ALU.mult, [bPS[6], br_lo], [br_lo])
            yield
            yield from wait_flags([("zact", n)])
            tt("pool", ybT[:, :, :], wt_, zB[:, :, :], ALU.mult, [br_lo, bzB], [byb])
            yield

        def join(n):
            I = tile_info(n)
            m = I["m"]
            yab = [gB[0]["ya"], gB[1]["ya"]]
            for br_, (src, bsrc, banks) in enumerate(((yaT, yab, (0, 1)), (ybT, [byb], (2, 3)))):
                for mo in range(8):
                    bank = banks[mo // 4]
                    for cc_ in range(4):
                        mm(PS[bank][:, (mo % 4) * 128:(mo % 4 + 1) * 128], Wb[:, br_, cc_, mo * 128:(mo + 1) * 128], src[:, cc_, :],
                           cc_ == 0, cc_ == 3, [bWb] + bsrc, [bPS[bank]])
            for half in range(2):
                stt(tmpa, gt[:, half * 4:half * 4 + 4, :], 1.0, P4(half), ALU.add, ALU.mult, [bPS[half], bgt], [bC])
                stt(tmpb, gt[:, 8 + half * 4:12 + half * 4, :], 1.0, P4(2 + half), ALU.add, ALU.mult, [bPS[2 + half], bgt], [bC])
                tt("pool", mg[:, half * 4:half * 4 + 4, :], tmpa, tmpb, ALU.add, [], [bmg, bC])
            for half in range(2):
                for kc in range(8):
                    mm(PS[4 + half][:, 0:512], mg[:, kc, :], Wo[:, kc, half * 512:(half + 1) * 512], kc == 0, kc == 7,
                       [bmg, bWo], [bPS[4 + half]])
            for half in range(2):
                stt(r32[:, half * 512:(half + 1) * 512], xr[:, half * 512:(half + 1) * 512], 2.0 * ALPHA, PS[4 + half][:, 0:512],
                    ALU.mult, ALU.add, [bxr, bPS[4 + half]], [br])
            S.op("dve", lambda e: e.reduce_sum(out=st[:, 0:1], in_=r32[:, :], axis=AX.X), reads=[br], writes=[bst])
            jk = tmpa.rearrange("p h t -> p (h t)")
            act(jk, r32[:, 0:512], AF.Square, [br], [bC, bst], accum=st[:, 1:2])
            act(jk, r32[:, 512:1024], AF.Square, [br], [bC, bst], accum=st[:, 2:3])
            S.op("dve", lambda e: e.tensor_scalar_mul(out=st[:, 3:4], in0=st[:, 0:1], scalar1=1.0 / 1024.0), reads=[bst], writes=[bst])
            tt("dve", st[:, 4:5], st[:, 1:2], st[:, 2:3], ALU.add, [bst], [bst])
            tt("dve", st[:, 5:6], st[:, 3:4], st[:, 3:4], ALU.mult, [bst], [bst])
            stt(st[:, 6:7], st[:, 4:5], 1.0 / 1024.0, st[:, 5:6], ALU.mult, ALU.subtract, [bst], [bst])
            act(st[:, 7:8], st[:, 6:7], AF.Ln, [bst], [bst], bias=4.0 * LN_EPS)
            act(st[:, 7:8], st[:, 7:8], AF.Exp, [bst], [bst], scale=-0.5)
            stt(st[:, 8:9], st[:, 3:4], -1.0, st[:, 7:8], ALU.mult, ALU.mult, [bst], [bst])
            act(r32[:, :], r32[:, :], AF.Identity, [br, bst], [br], bias=st[:, 8:9], scale=st[:, 7:8])
            tt("dve", r32[:, 0:512], r32[:, 0:512], lng[:, 0:512], ALU.mult, [br, blng], [br])
            tt("pool", r32[:, 512:1024], r32[:, 512:1024], lng[:, 512:1024], ALU.mult, [br, blng], [br])
            tt("dve", r32[:, 0:512], r32[:, 0:512], lnb[:, 0:512], ALU.add, [br, blnb], [br])
            tt("pool", r32[:, 512:1024], r32[:, 512:1024], lnb[:, 512:1024], ALU.add, [br, blnb], [br])
            S.op("sp", lambda e: e.dma_start(out=out_d[m], in_=r32[:, :]), reads=[br], writes=[buf("o%d" % m)], slot=stq)

        def run_streams(streams, weights=None):
            weights = weights or {}
            while streams:
                for gen in list(streams):
                    for _ in range(weights.get(id(gen), 1)):
                        try:
                            next(gen)
                        except StopIteration:
                            streams.remove(gen)
                            break

        S.op("sp", lambda e: e.dma_start(out=xfb[:, :], in_=xT_d[0]), writes=[bxf], slot=ldx)
        cast_tile(0)
        run_streams([early(0)])
        for n in range(NT):
            I = tile_info(n)
            if STOP is not None and STOP <= n:
                break
            if n + 1 < NT:
                cast_tile(n + 1)
            streams = [group(n, 0), group(n, 1)]
            if I["is_main"]:
                streams.append(late_front(n, streams))
            wts = {}
            if n + 1 < NT:
                eg = early(n + 1)
                streams.append(eg)
                wts[id(eg)] = EARLY_W
            run_streams(streams, wts)
            if I["is_main"]:
                join(n)

        if STOP is not None:
            for m in range(NMAIN):
                S.op("sp", lambda e, m=m: e.dma_start(out=out_d[m], in_=lng[:, :]), reads=[blng], writes=[buf("o%d" % m)], slot=stq)
        S.final_wait("sp", [B["o%d" % m] for m in range(NMAIN)])
        S.emit()
    return nc


def _consts(first_half):
    i = np.arange(128)
    same = (i[:, None] // 64) == (i[None, :] // 64)
    c32 = np.zeros((128, NC32), np.float32)
    c32[:, C_ID:C_ID + 128] = np.eye(128, dtype=np.float32)
    c32[:, C_ONE:C_ONE + 128] = 1.0
    c32[:, C_TRI:C_TRI + 128] = ((i[:, None] <= i[None, :]) & same)
    c32[:, C_SOWN:C_SOWN + 128] = same
    c32[:, C_SA0:C_SA0 + 128] = (i[:, None] < 64)
    c32[:, C_SA1:C_SA1 + 128] = (i[:, None] >= 64)
    c16 = np.zeros((128, NC16), np.float32)
    c16[:, K_ID:K_ID + 128] = np.eye(128, dtype=np.float32)
    c16[:, K_ONE:K_ONE + 64] = 1.0
    p, f = i[:, None], i[None, :]
    c16[:, K_MS:K_MS + 128] = np.where((p > f) & same, 0.0, NEG)
    c16[:, K_MST:K_MST + 128] = np.where((f > p) & same, 0.0, NEG)
    c16[:, K_MU:K_MU + 128] = np.where((f >= p) & same, 0.0, NEG)
    c16[:, K_MC:K_MC + 128] = np.where(f >= p, 0.0, NEG)
    mp = np.where(p > f, 0.0, NEG)
    c16[:, K_MP:K_MP + 128] = mp
    c16[:, K_MP0:K_MP0 + 128] = NEG if first_half else mp
    return c32, c16


def _rope_table(pos0, nsw):
    inv_freq = (np.float32(500000.0) ** (-np.arange(0, 16, 2, dtype=np.float32) / np.float32(16))).astype(np.float32)
    pos = (pos0 + np.arange(nsw * 128)).astype(np.float32)
    ang = (pos[:, None] * inv_freq[None, :]).astype(np.float32)
    c, s = np.cos(ang).astype(np.float32), np.sin(ang).astype(np.float32)
    tab = np.concatenate([c, c, -s, s], axis=1).reshape(nsw, 128, 32)
    return np.ascontiguousarray(tab.transpose(1, 0, 2)).reshape(128, nsw * 32)


def _tile_w(w, cols):
    ws = w[:, cols]
    return np.ascontiguousarray(ws.reshape(8, 128, -1).transpose(1, 0, 2)).reshape(128, -1)


def prepare_inputs(x, w_in, conv_w, a_log, dt_bias, dn_norm_w, sinks, w_branch, w_out, ln_g, ln_b, npre, nmain):
    x = np.asarray(x, np.float32)
    nb, seq, _ = x.shape
    T = nmain * 128
    assert seq == 2 * T and npre == nmain
    w_in = np.asarray(w_in, np.float32)[0]
    colf = np.concatenate([np.arange(O_DQ, O_DZ + 512), np.arange(O_SZ, O_SZ + 512), np.arange(O_GA, O_GB + 1024)])
    colt = np.concatenate([np.arange(O_SQ, O_SQ + 512), np.arange(O_SK, O_SK + 128), np.arange(O_SV, O_SV + 128),
                           np.arange(O_DB, O_DB + 4), np.arange(O_DA, O_DA + 4)])
    assert colf.size == NF and colt.size == NTM
    wf = _tile_w(w_in, colf)
    wt = _tile_w(w_in, colt)
    wbr = np.asarray(w_branch, np.float32)[0]
    wb = np.ascontiguousarray(wbr.reshape(2, 4, 128, 1024).transpose(2, 0, 1, 3)).reshape(128, 8192)
    wo = np.ascontiguousarray(np.asarray(w_out, np.float32)[0].reshape(8, 128, 1024).transpose(1, 0, 2)).reshape(128, 8192)
    cw = np.asarray(conv_w, np.float32)[0]
    convw = np.ascontiguousarray(cw.reshape(4, 12, 128).transpose(2, 1, 0)).reshape(128, 48)
    small = np.zeros((128, 12), np.float32)
    small[:, 0:4] = np.asarray(a_log, np.float32)[0][None, :]
    small[:, 4:8] = np.asarray(dt_bias, np.float32)[0][None, :]
    sk = np.asarray(sinks, np.float32)[0]
    for c in range(4):
        small[0:64, 8 + c] = sk[2 * c]
        small[64:128, 8 + c] = sk[2 * c + 1]
    normw = np.asarray(dn_norm_w, np.float32)[0][None, :]
    lng = np.asarray(ln_g, np.float32)[0][None, :]
    lnb = np.asarray(ln_b, np.float32)[0][None, :]
    in_maps = []
    for b in range(nb):
        for h in range(2):
            start = h * T
            xa = np.zeros((npre * 128 + T, D), np.float32)
            if h == 1:
                xa[:] = x[b, 0:2 * T]
            else:
                xa[npre * 128:] = x[b, 0:T]
            nt = npre + nmain
            xT = np.ascontiguousarray(xa.reshape(nt, 128, 8, 128).transpose(0, 3, 2, 1)).reshape(nt, 128, 1024)
            xr = np.ascontiguousarray(x[b, start:start + T].reshape(nmain, 128, D))
            c32, c16 = _consts(h == 0)
            rope = _rope_table(start - 128, nmain + 1)
            in_maps.append({"xT": xT, "xr": xr, "wf": wf, "wt": wt, "wb": wb, "wo": wo, "c32": c32, "c16": c16,
                            "convw": convw, "small": small, "normw": normw, "lng": lng, "lnb": lnb, "rope": rope})
    return in_maps


_NC_CACHE = {}


def run(inputs, npre, nmain):
    in_maps = prepare_inputs(npre=npre, nmain=nmain, **inputs)
    key = (npre, nmain)
    if key not in _NC_CACHE:
        _NC_CACHE[key] = build(npre, nmain)
    nc = _NC_CACHE[key]
    ncores = len(in_maps)
    res = run_bass_kernel_spmd(nc, in_maps, core_ids=list(range(ncores)))
    nb = ncores // 2
    T = nmain * 128
    out = np.zeros((nb, 2 * T, D), np.float32)
    for b in range(nb):
        for h in range(2):
            out[b, h * T:(h + 1) * T] = res.results[b * 2 + h]["out"].reshape(T, D)
    return out


def kernel(x, w_in, conv_w, a_log, dt_bias, dn_norm_w, sinks, w_branch, w_out, ln_g, ln_b):
    inputs = dict(x=x, w_in=w_in, conv_w=conv_w, a_log=a_log, dt_bias=dt_bias, dn_norm_w=dn_norm_w, sinks=sinks,
                  w_branch=w_branch, w_out=w_out, ln_g=ln_g, ln_b=ln_b)
    return run(inputs, 32, 32)
```
